# Optimizing a Trainium2 kernel written in Bass

```python
import math
import jax, jax.numpy as jnp
from jax import lax
import numpy as np

D_MODEL = 1024
BATCH = 8
SEQ = 2048
DEPTH = 1
DEC_BATCH = 128
DEC_SEQ = 8
PAST_LEN = 16384
PAGE_SIZE = 128

D_FF = 2816
POOL_WINDOWS = (2, 4, 8, 16)
N_POOL_GROUPS = 4
POOL_WIDTH = D_MODEL
POOL_GROUP = POOL_WIDTH // N_POOL_GROUPS
POOL_HIST = 16 - 1
SSD_EXPAND = 2
D_INNER = SSD_EXPAND * D_MODEL
SSD_HEAD_DIM = 64
N_SSD_HEADS = D_INNER // SSD_HEAD_DIM
N_SSD_GROUPS = 4
HEADS_PER_GROUP = N_SSD_HEADS // N_SSD_GROUPS
D_STATE = 128
CONV_WIDTH = 4
BC_DIM = N_SSD_GROUPS * D_STATE
CONV_DIM = D_INNER + 2 * BC_DIM
CHUNK = 128
N_BRANCHES = 2
IN_PROJ_DIM = POOL_WIDTH + D_INNER + CONV_DIM + N_SSD_HEADS + N_BRANCHES * D_MODEL
EPS = 1e-6

kernel_name = 'hybrid_pool_ssd_macaron_step'


def rmsnorm(x, g):
    xf = x.astype(jnp.float32)
    y = xf * lax.rsqrt(jnp.mean(xf * xf, axis=-1, keepdims=True) + EPS)
    return (y * g.astype(jnp.float32)).astype(x.dtype)


def swiglu(x, w_in, w_out):
    gu = x @ w_in
    g, u = gu[..., :D_FF], gu[..., D_FF:]
    return (jax.nn.silu(g) * u) @ w_out


def pool_mix(u_ext, T, out_pos0, w_group, scale):
    Bsz = u_ext.shape[0]
    f32 = jnp.float32
    uf = u_ext.astype(f32)
    cs = jnp.concatenate([jnp.zeros((Bsz, 1, POOL_WIDTH), f32), jnp.cumsum(uf, axis=1)], axis=1)
    pos = out_pos0 + jnp.arange(T)
    means = []
    for gi, w in enumerate(POOL_WINDOWS):
        c0, c1 = gi * POOL_GROUP, (gi + 1) * POOL_GROUP
        s = cs[:, POOL_HIST + 1:, c0:c1] - cs[:, POOL_HIST + 1 - w:POOL_HIST + 1 - w + T, c0:c1]
        cnt = jnp.minimum(pos + 1, w).astype(f32)
        means.append(s / cnt[None, :, None])
    d = jnp.concatenate(means, axis=-1) - uf[:, POOL_HIST:]
    d = d.astype(u_ext.dtype).reshape(Bsz, T, N_POOL_GROUPS, POOL_GROUP)
    y = jnp.einsum('btgc,gcd->btgd', d, w_group).reshape(Bsz, T, POOL_WIDTH)
    return y * scale


def causal_conv(xbc, hist, w, b):
    T = xbc.shape[1]
    xp = jnp.concatenate([hist, xbc], axis=1)
    y = b + xp[:, 0:T] * w[0]
    for k in range(1, CONV_WIDTH):
        y = y + xp[:, k:k + T] * w[k]
    return jax.nn.silu(y), xp[:, -(CONV_WIDTH - 1):]


def ssd_scan(x, dt, A, bm, cm, h0):
    f32 = jnp.float32
    Bsz, T = x.shape[0], x.shape[1]
    L = min(CHUNK, T)
    pad = (-T) % L
    if pad:
        padf = lambda a: jnp.pad(a, [(0, 0), (0, pad)] + [(0, 0)] * (a.ndim - 2))
        x, dt, bm, cm = padf(x), padf(dt), padf(bm), padf(cm)
    Tp = T + pad
    nc = Tp // L
    G, R, P, N = N_SSD_GROUPS, HEADS_PER_GROUP, SSD_HEAD_DIM, D_STATE
    xr = x.astype(f32).reshape(Bsz, nc, L, G, R, P)
    dtr = dt.astype(f32).reshape(Bsz, nc, L, G, R)
    br = bm.astype(f32).reshape(Bsz, nc, L, G, N)
    cr = cm.astype(f32).reshape(Bsz, nc, L, G, N)
    a_cs = jnp.cumsum(dtr * A.reshape(G, R), axis=2)
    xdt = xr * dtr[..., None]
    acs_t = jnp.moveaxis(a_cs, 2, -1)
    causal = jnp.tril(jnp.ones((L, L), dtype=bool))
    seg = jnp.where(causal, acs_t[..., :, None] - acs_t[..., None, :], -jnp.inf)
    decay = jnp.exp(seg)
    cb = jnp.einsum('bclgn,bcsgn->bcgls', cr, br)
    y_diag = jnp.einsum('bcgrls,bcsgrp->bclgrp', cb[:, :, :, None] * decay, xdt)
    decay_end = jnp.exp(a_cs[:, :, -1:] - a_cs)
    st = jnp.einsum('bclgn,bclgr,bclgrp->bcgrpn', br, decay_end, xdt)
    chunk_decay = jnp.exp(a_cs[:, :, -1])

    def step(h, inp):
        s_c, d_c = inp
        return h * d_c[..., None, None] + s_c, h

    h_last, h_prev = lax.scan(step, h0.astype(f32).reshape(Bsz, G, R, P, N),
                              (jnp.moveaxis(st, 1, 0), jnp.moveaxis(chunk_decay, 1, 0)))
    h_prev = jnp.moveaxis(h_prev, 0, 1)
    y_off = jnp.einsum('bclgn,bcgrpn,bclgr->bclgrp', cr, h_prev, jnp.exp(a_cs))
    y = (y_diag + y_off).reshape(Bsz, Tp, N_SSD_HEADS, P)[:, :T]
    return y, h_last.reshape(Bsz, N_SSD_HEADS, P, N)


def token_mixing(hn, pool_hist, out_pos0, conv_hist, ssm_h0, w_in, pool_w_group, pool_scale,
                 pool_w_out, conv_w, conv_b, dt_bias, a_log, d_skip, ssd_norm, ssd_w_out, w_o):
    f32 = jnp.float32
    Bsz, T, _ = hn.shape
    proj = hn @ w_in
    s1 = POOL_WIDTH
    s2 = s1 + D_INNER
    s3 = s2 + CONV_DIM
    s4 = s3 + N_SSD_HEADS
    u, z, xbc, dt_raw, gate_logits = (proj[..., :s1], proj[..., s1:s2], proj[..., s2:s3],
                                      proj[..., s3:s4], proj[..., s4:])
    u_ext = jnp.concatenate([pool_hist.astype(u.dtype), u], axis=1)
    branch_pool = pool_mix(u_ext, T, out_pos0, pool_w_group, pool_scale) @ pool_w_out
    new_pool = u_ext[:, -POOL_HIST:]
    xbc_c, new_conv = causal_conv(xbc, conv_hist.astype(xbc.dtype), conv_w, conv_b)
    xs = xbc_c[..., :D_INNER].reshape(Bsz, T, N_SSD_HEADS, SSD_HEAD_DIM)
    bm = xbc_c[..., D_INNER:D_INNER + BC_DIM].reshape(Bsz, T, N_SSD_GROUPS, D_STATE)
    cm = xbc_c[..., D_INNER + BC_DIM:].reshape(Bsz, T, N_SSD_GROUPS, D_STATE)
    dt = jax.nn.softplus(dt_raw.astype(f32) + dt_bias.astype(f32))
    A = -jnp.exp(a_log.astype(f32))
    y, new_ssm = ssd_scan(xs, dt, A, bm, cm, ssm_h0)
    y = y + d_skip.astype(f32)[:, None] * xs.astype(f32)
    yg = (y.reshape(Bsz, T, D_INNER) * jax.nn.silu(z.astype(f32)))
    yg = yg.reshape(Bsz, T, N_SSD_GROUPS, D_INNER // N_SSD_GROUPS)
    yg = yg * lax.rsqrt(jnp.mean(yg * yg, axis=-1, keepdims=True) + EPS)
    y = (yg.reshape(Bsz, T, D_INNER) * ssd_norm.astype(f32)).astype(hn.dtype)
    branch_ssd = y @ ssd_w_out
    gates = jax.nn.sigmoid(gate_logits.astype(f32)).reshape(Bsz, T, N_BRANCHES, D_MODEL)
    merged = (gates[:, :, 0] * branch_pool.astype(f32) + gates[:, :, 1] * branch_ssd.astype(f32)).astype(hn.dtype)
    return merged @ w_o, new_pool, new_conv, new_ssm


def setup_inputs(seed: int = 0) -> dict:
    key = jax.random.key(seed)
    ks = iter(jax.random.split(key, 32))
    nrm = lambda shape, scale: scale * jax.random.normal(next(ks), shape, jnp.float32)
    gain = lambda shape: 1.0 + nrm(shape, 0.05)
    Lr = DEPTH
    x_prompt = nrm((BATCH, SEQ, D_MODEL), 1.0)
    x_sample = nrm((DEC_BATCH, DEC_SEQ, D_MODEL), 1.0)
    state_pool = nrm((Lr, DEC_BATCH, POOL_HIST, POOL_WIDTH), 1.0)
    state_conv = nrm((Lr, DEC_BATCH, CONV_WIDTH - 1, CONV_DIM), 1.0)
    state_ssm = nrm((Lr, DEC_BATCH, N_SSD_HEADS, SSD_HEAD_DIM, D_STATE), 0.5)
    norm_ffn1 = gain((Lr, D_MODEL))
    ffn1_w_in = nrm((Lr, D_MODEL, 2 * D_FF), D_MODEL ** -0.5)
    ffn1_w_out = nrm((Lr, D_FF, D_MODEL), D_FF ** -0.5)
    norm_mix = gain((Lr, D_MODEL))
    w_in = nrm((Lr, D_MODEL, IN_PROJ_DIM), D_MODEL ** -0.5)
    pool_w_group = nrm((Lr, N_POOL_GROUPS, POOL_GROUP, POOL_GROUP), POOL_GROUP ** -0.5)
    pool_scale = gain((Lr, POOL_WIDTH))
    pool_w_out = nrm((Lr, POOL_WIDTH, D_MODEL), POOL_WIDTH ** -0.5)
    conv_w = nrm((Lr, CONV_WIDTH, CONV_DIM), CONV_WIDTH ** -0.5)
    conv_b = nrm((Lr, CONV_DIM), 0.02)
    dt0 = jnp.exp(jax.random.uniform(next(ks), (Lr, N_SSD_HEADS), jnp.float32,
                                     minval=math.log(1e-3), maxval=math.log(1e-1)))
    dt_bias = dt0 + jnp.log(-jnp.expm1(-dt0))
    a_log = jnp.log(jax.random.uniform(next(ks), (Lr, N_SSD_HEADS), jnp.float32, minval=1.0, maxval=16.0))
    d_skip = gain((Lr, N_SSD_HEADS))
    ssd_norm = gain((Lr, D_INNER))
    ssd_w_out = nrm((Lr, D_INNER, D_MODEL), D_INNER ** -0.5)
    w_o = nrm((Lr, D_MODEL, D_MODEL), D_MODEL ** -0.5)
    norm_ffn2 = gain((Lr, D_MODEL))
    ffn2_w_in = nrm((Lr, D_MODEL, 2 * D_FF), D_MODEL ** -0.5)
    ffn2_w_out = nrm((Lr, D_FF, D_MODEL), D_FF ** -0.5)
    norm_final = gain((D_MODEL,))
    return {'x_prompt': x_prompt, 'x_sample': x_sample, 'state_pool': state_pool,
            'state_conv': state_conv, 'state_ssm': state_ssm, 'norm_ffn1': norm_ffn1,
            'ffn1_w_in': ffn1_w_in, 'ffn1_w_out': ffn1_w_out, 'norm_mix': norm_mix, 'w_in': w_in,
            'pool_w_group': pool_w_group, 'pool_scale': pool_scale, 'pool_w_out': pool_w_out,
            'conv_w': conv_w, 'conv_b': conv_b, 'dt_bias': dt_bias, 'a_log': a_log,
            'd_skip': d_skip, 'ssd_norm': ssd_norm, 'ssd_w_out': ssd_w_out, 'w_o': w_o,
            'norm_ffn2': norm_ffn2, 'ffn2_w_in': ffn2_w_in, 'ffn2_w_out': ffn2_w_out,
            'norm_final': norm_final}


def reference(x_prompt, x_sample, state_pool, state_conv, state_ssm, norm_ffn1, ffn1_w_in,
              ffn1_w_out, norm_mix, w_in, pool_w_group, pool_scale, pool_w_out, conv_w, conv_b,
              dt_bias, a_log, d_skip, ssd_norm, ssd_w_out, w_o, norm_ffn2, ffn2_w_in,
              ffn2_w_out, norm_final):
    def layer(x, pool_hist, out_pos0, conv_hist, h0, l):
        x = x + 0.5 * swiglu(rmsnorm(x, norm_ffn1[l]), ffn1_w_in[l], ffn1_w_out[l])
        mix, new_pool, new_conv, new_ssm = token_mixing(
            rmsnorm(x, norm_mix[l]), pool_hist, out_pos0, conv_hist, h0, w_in[l], pool_w_group[l],
            pool_scale[l], pool_w_out[l], conv_w[l], conv_b[l], dt_bias[l], a_log[l], d_skip[l],
            ssd_norm[l], ssd_w_out[l], w_o[l])
        x = x + mix
        x = x + 0.5 * swiglu(rmsnorm(x, norm_ffn2[l]), ffn2_w_in[l], ffn2_w_out[l])
        return x, new_pool, new_conv, new_ssm

    xp, xs = x_prompt, x_sample
    pp, cp, sp, ps, cs, ss = [], [], [], [], [], []
    for l in range(DEPTH):
        xp, a, b, c = layer(xp, jnp.zeros((BATCH, POOL_HIST, POOL_WIDTH), xp.dtype), 0,
                            jnp.zeros((BATCH, CONV_WIDTH - 1, CONV_DIM), xp.dtype),
                            jnp.zeros((BATCH, N_SSD_HEADS, SSD_HEAD_DIM, D_STATE), jnp.float32), l)
        pp.append(a); cp.append(b); sp.append(c)
        xs, d, e, f = layer(xs, state_pool[l], PAST_LEN, state_conv[l], state_ssm[l], l)
        ps.append(d); cs.append(e); ss.append(f)
    y_prompt = rmsnorm(xp, norm_final)
    y_sample = rmsnorm(xs, norm_final)
    return (y_prompt, y_sample, jnp.stack(pp), jnp.stack(cp), jnp.stack(sp),
            jnp.stack(ps), jnp.stack(cs), jnp.stack(ss))
```

```python
import contextlib
import os
import numpy as np
import concourse.bass as bass
import concourse.mybir as mybir
from concourse.bass_utils import run_bass_kernel_spmd

F32 = mybir.dt.float32
BF16 = mybir.dt.bfloat16
ALU = mybir.AluOpType
AF = mybir.ActivationFunctionType

ENGS = ("pe", "act", "dve", "pool", "sp")
SEM_ROT = 30000


class Trk:
    def __init__(self, nc, stack):
        self.nc = nc
        self.stack = stack
        self.eng = {"pe": nc.tensor, "act": nc.scalar, "dve": nc.vector,
                    "pool": nc.gpsimd, "sp": nc.sync}
        self.sem = {}
        self.cnt = {}
        self.nsem = 0
        for e in ENGS:
            self._new_eng_sem(e)
        self.waited = {e: {} for e in ENGS}
        self.last_w = {}
        self.readers = {}
        self.dma_groups = {}
        self.n_wait = 0
        self.n_ops = 0

    def _alloc_sem(self, name):
        s = self.stack.enter_context(self.nc.semaphore(name))
        self.nsem += 1
        return s

    def _new_eng_sem(self, e):
        self.sem[e] = self._alloc_sem(f"s_{e}_{self.nsem}")
        self.cnt[e] = 0

    def _deps(self, eng, reads, writes):
        deps = []
        for k in reads:
            t = self.last_w.get(k)
            if t is not None:
                if t[2] != eng or eng in ("act", "dve", "pool", "dma"):
                    deps.append(t)
        for k in writes:
            t = self.last_w.get(k)
            if t is not None and (t[2] != eng or eng == "dma"):
                deps.append(t)
            for t in self.readers.get(k, {}).values():
                if t[2] != eng or eng == "dma":
                    deps.append(t)
        return deps

    def _emit_waits(self, eng, deps):
        w = self.waited[eng]
        best = {}
        for (s, v, _) in deps:
            sid = id(s)
            if w.get(sid, 0) >= v:
                continue
            if sid not in best or best[sid][1] < v:
                best[sid] = (s, v)
        for sid, (s, v) in best.items():
            self.eng[eng].wait_ge(s, v)
            w[sid] = v
            self.n_wait += 1

    def _record(self, tok, reads, writes):
        for k in writes:
            self.last_w[k] = tok
            self.readers[k] = {}
        for k in reads:
            r = self.readers.setdefault(k, {})
            key = id(tok[0])
            if key not in r or r[key][1] < tok[1]:
                r[key] = tok

    def op(self, eng, fn, reads=(), writes=(), inc=True):
        self.n_ops += 1
        deps = self._deps(eng, reads, writes)
        self._emit_waits(eng, deps)
        ins = fn()
        if inc:
            if self.cnt[eng] >= SEM_ROT:
                self._new_eng_sem(eng)
            self.cnt[eng] += 1
            ins.then_inc(self.sem[eng], 1)
            tok = (self.sem[eng], self.cnt[eng], eng)
        else:
            assert self.cnt[eng] < SEM_ROT
            tok = (self.sem[eng], self.cnt[eng] + 1, eng)
        self._record(tok, reads, writes)
        return ins

    def dma(self, q, out, in_, reads=(), writes=(), group=None, **kw):
        self.n_ops += 1
        deps = self._deps("dma", reads, writes)
        self._emit_waits(q, deps)
        g = self.dma_groups.get(group)
        if g is None:
            g = [self._alloc_sem(f"d_{self.nsem}"), 0]
            self.dma_groups[group] = g
        ins = self.eng[q].dma_start(out=out, in_=in_, **kw)
        g[1] += 16
        ins.then_inc(g[0], 16)
        tok = (g[0], g[1], "dma")
        self._record(tok, reads, writes)
        return ins

    def all_tokens(self, skip_groups=()):
        toks = []
        for e in ENGS:
            if self.cnt[e] > 0:
                toks.append((self.sem[e], self.cnt[e], e))
        for k, g in self.dma_groups.items():
            if g[1] > 0 and not (isinstance(k, tuple) and k and k[0] in skip_groups):
                toks.append((g[0], g[1], "dma"))
        return toks

    def barrier(self, skip_groups=("w",)):
        toks = self.all_tokens(skip_groups)
        for e in ENGS:
            self._emit_waits(e, toks)

    def final_wait(self):
        self._emit_waits("sp", self.all_tokens())


D = 1024
KC = 8
DFF = 2816
NJ = 22
DIN = 2048
CONVD = 3072
NH = 32
HP = 64
NST = 128
NG = 4
IN_PROJ = 8224
OFF_U, OFF_Z, OFF_X, OFF_B, OFF_C, OFF_DT, OFF_G0, OFF_G1 = 0, 1024, 3072, 5120, 5632, 6144, 6176, 7200
EPS = 1e-6
NPT = 16
STILE = 16
NSEQ = 16
TS = 8
PASSES = [list(range(0, 6)), list(range(6, 12)), list(range(12, 17))]
NTMAX = 768
NSLOT = 3
SLOTW = 5632
NEG = -30000.0

CFG = {
    "mix": int(os.environ.get("K_MIX", "1")),
    "ffn1": int(os.environ.get("K_FFN1", "1")),
    "ffn2": int(os.environ.get("K_FFN2", "1")),
}


def build_nc(cfg=CFG):
    nc = bass.Bass("TRN2", target_bir_lowering=False)

    def din(name, shape):
        return nc.dram_tensor(name, list(shape), F32, kind="ExternalInput").ap()

    def dout(name, shape):
        return nc.dram_tensor(name, list(shape), F32, kind="ExternalOutput").ap()

    xp = din("xp", [2048, D])
    xs = din("xs", [128, D])
    st_pool = din("st_pool", [NSEQ, 15, D])
    st_conv = din("st_conv", [NSEQ, 3, CONVD])
    st_ssm = din("st_ssm", [NSEQ, DIN, NST])
    norm_ffn1 = din("norm_ffn1", [D])
    ffn1_w_in = din("ffn1_w_in", [D, 2 * DFF])
    ffn1_w_out = din("ffn1_w_out", [DFF, D])
    norm_mix = din("norm_mix", [D])
    w_in = din("w_in", [D, IN_PROJ])
    pool_w_group = din("pool_w_group", [NG, 256, 256])
    pool_scale = din("pool_scale", [D])
    pool_w_out = din("pool_w_out", [D, D])
    conv_w = din("conv_w", [4, CONVD])
    conv_b = din("conv_b", [CONVD])
    dt_bias = din("dt_bias", [NH])
    a_log = din("a_log", [NH])
    d_skip = din("d_skip", [NH])
    ssd_norm = din("ssd_norm", [DIN])
    ssd_w_out = din("ssd_w_out", [DIN, D])
    w_o = din("w_o", [D, D])
    norm_ffn2 = din("norm_ffn2", [D])
    ffn2_w_in = din("ffn2_w_in", [D, 2 * DFF])
    ffn2_w_out = din("ffn2_w_out", [DFF, D])
    norm_final = din("norm_final", [D])

    y_p = dout("y_p", [2048, D])
    y_s = dout("y_s", [128, D])
    o_pool_p = dout("o_pool_p", [15, D])
    o_conv_p = dout("o_conv_p", [3, CONVD])
    o_ssm_p = dout("o_ssm_p", [DIN, NST])
    o_pool_s = dout("o_pool_s", [NSEQ, 15, D])
    o_conv_s = dout("o_conv_s", [NSEQ, 3, CONVD])
    o_ssm_s = dout("o_ssm_s", [NSEQ, DIN, NST])

    with contextlib.ExitStack() as st:
        T = Trk(nc, st)
        V, A_, P_ = nc.vector, nc.scalar, nc.gpsimd

        uniq = [0]

        def sb(name, shape, dt=F32, stack=st):
            uniq[0] += 1
            return stack.enter_context(nc.sbuf_tensor(f"{name}_{uniq[0]}", list(shape), dt))

        psf = [st.enter_context(nc.psum_tensor(f"psf{i}", [128, 512], F32)) for i in range(6)]
        psb = [st.enter_context(nc.psum_tensor(f"psb{i}", [128, 1024], BF16)) for i in range(2)]
        rr = {"f": 0, "b": 0}
        held = set()

        def nbf(hold=False):
            while True:
                i = rr["f"] % 6
                rr["f"] += 1
                if i not in held:
                    break
            if hold:
                held.add(i)
            return psf[i], ("psf", i)

        def release(key):
            held.discard(key[1])

        def nbb():
            i = rr["b"] % 2
            rr["b"] += 1
            return psb[i], ("psb", i)

        ones_f = sb("ones_f", [128, 128])
        zeros_f = sb("zeros_f", [128, 128])
        ident_f = sb("ident_f", [128, 128])
        ident_b = sb("ident_b", [128, 128], BF16)
        ones_b = sb("ones_b", [128, 128], BF16)
        tri_f = sb("tri_f", [128, 2, 128])
        tri_b = sb("tri_b", [128, 2, 128], BF16)
        negm = sb("negm", [128, 2, 4, 128], BF16)
        negm_f = sb("negm_f", [128, 2, 128])
        smask = sb("smask", [128, NSEQ, 128], BF16)
        rowmask = sb("rowmask", [128, NSEQ])
        gcols = sb("gcols", [128, 3, 8])
        pscol = sb("pscol", [128, 8])
        cwcol = sb("cwcol", [128, 4, 24])
        cbcol = sb("cbcol", [128, 24])
        blk_f = sb("blk_f", [128, 128])
        dtb_b = sb("dtb_b", [128, NH])
        A_b = sb("A_b", [128, NH])
        D_b = sb("D_b", [128, NH])
        invc = sb("invc", [128, 16])
        eps_col = sb("eps_col", [128, 2])
        Ddiag = sb("Ddiag", [128, NH, 128], BF16)

        def pool_op(fn, reads, writes):
            return T.op("pool", fn, reads, writes)

        pool_op(lambda: P_.memset(ones_f[:], 1.0), [], ["ones_f"])
        pool_op(lambda: P_.memset(zeros_f[:], 0.0), [], ["zeros_f"])
        pool_op(lambda: P_.memset(ones_b[:], 1.0), [], ["ones_b"])
        pool_op(lambda: P_.memset(eps_col[:, 0:1], EPS), [], ["eps_col"])
        pool_op(lambda: P_.memset(eps_col[:, 1:2], 4.0 * EPS), [], ["eps_col"])
        pool_op(lambda: P_.affine_select(out=ident_f[:], in_=ones_f[:], pattern=[[-1, 128]],
                                         compare_op=ALU.is_equal, fill=0.0, base=0, channel_multiplier=1),
                ["ones_f"], ["ident_f"])
        pool_op(lambda: P_.tensor_copy(out=ident_b[:], in_=ident_f[:]), ["ident_f"], ["ident_b"])
        pool_op(lambda: P_.affine_select(out=tri_f[:, 0, :], in_=ones_f[:], pattern=[[1, 128]],
                                         compare_op=ALU.is_ge, fill=0.0, base=0, channel_multiplier=-1),
                ["ones_f"], ["tri_f"])
        pool_op(lambda: P_.affine_select(out=negm_f[:, 0, :], in_=zeros_f[:], pattern=[[1, 128]],
                                         compare_op=ALU.is_ge, fill=NEG, base=0, channel_multiplier=-1),
                ["zeros_f"], ["negm_f"])
        t3 = tri_f[:, 1, :].rearrange("p (a b) -> p a b", b=8)
        n3 = negm_f[:, 1, :].rearrange("p (a b) -> p a b", b=8)
        o3 = ones_f[:].rearrange("p (a b) -> p a b", b=8)
        z3 = zeros_f[:].rearrange("p (a b) -> p a b", b=8)
        pool_op(lambda: P_.affine_select(out=t3, in_=o3, pattern=[[8, 16], [1, 8]],
                                         compare_op=ALU.is_ge, fill=0.0, base=0, channel_multiplier=-1),
                ["ones_f"], ["tri_f"])
        pool_op(lambda: P_.affine_select(out=t3, in_=t3, pattern=[[-8, 16], [0, 8]],
                                         compare_op=ALU.is_ge, fill=0.0, base=0, channel_multiplier=1),
                ["tri_f"], ["tri_f"])
        pool_op(lambda: P_.affine_select(out=n3, in_=z3, pattern=[[8, 16], [1, 8]],
                                         compare_op=ALU.is_ge, fill=NEG, base=0, channel_multiplier=-1),
                ["zeros_f"], ["negm_f"])
        pool_op(lambda: P_.affine_select(out=n3, in_=n3, pattern=[[-8, 16], [0, 8]],
                                         compare_op=ALU.is_ge, fill=NEG, base=0, channel_multiplier=1),
                ["negm_f"], ["negm_f"])
        pool_op(lambda: P_.tensor_copy(out=tri_b[:], in_=tri_f[:]), ["tri_f"], ["tri_b"])
        for k in range(2):
            pool_op(lambda k=k: P_.tensor_copy(out=negm[:, k],
                                               in_=negm_f[:, k, :].unsqueeze(1).to_broadcast([128, 4, 128])),
                    ["negm_f"], ["negm"])
        pool_op(lambda: P_.memset(smask[:], 1.0), [], ["smask"])
        pool_op(lambda: P_.affine_select(out=smask[:], in_=smask[:], pattern=[[-8, NSEQ], [1, 128]],
                                         compare_op=ALU.is_ge, fill=0.0, base=0, channel_multiplier=0),
                ["smask"], ["smask"])
        pool_op(lambda: P_.affine_select(out=smask[:], in_=smask[:], pattern=[[8, NSEQ], [-1, 128]],
                                         compare_op=ALU.is_ge, fill=0.0, base=7, channel_multiplier=0),
                ["smask"], ["smask"])
        b3 = blk_f[:].rearrange("p (a b) -> p a b", b=8)
        pool_op(lambda: P_.affine_select(out=b3, in_=o3, pattern=[[-8, 16], [0, 8]],
                                         compare_op=ALU.is_ge, fill=0.0, base=0, channel_multiplier=1),
                ["ones_f"], ["blk_f"])
        pool_op(lambda: P_.affine_select(out=b3, in_=b3, pattern=[[8, 16], [0, 8]],
                                         compare_op=ALU.is_ge, fill=0.0, base=7, channel_multiplier=-1),
                ["blk_f"], ["blk_f"])
        pool_op(lambda: P_.memset(rowmask[:], 1.0), [], ["rowmask"])
        pool_op(lambda: P_.affine_select(out=rowmask[:], in_=rowmask[:], pattern=[[-8, NSEQ]],
                                         compare_op=ALU.is_ge, fill=0.0, base=0, channel_multiplier=1),
                ["rowmask"], ["rowmask"])
        pool_op(lambda: P_.affine_select(out=rowmask[:], in_=rowmask[:], pattern=[[8, NSEQ]],
                                         compare_op=ALU.is_ge, fill=0.0, base=7, channel_multiplier=-1),
                ["rowmask"], ["rowmask"])
        pool_op(lambda: P_.iota(invc[:], [[1, 16]], base=1, channel_multiplier=0,
                                allow_small_or_imprecise_dtypes=True), [], ["invc"])
        T.op("dve", lambda: V.reciprocal(out=invc[:], in_=invc[:]), ["invc"], ["invc"])

        with nc.allow_non_contiguous_dma(reason="small param columns"):
            for i, g in enumerate((norm_ffn1, norm_mix, norm_ffn2)):
                T.dma("sp", gcols[:, i, :], g.rearrange("(c p) -> p c", p=128), writes=["gcols"], group=("c", 0))
            T.dma("sp", pscol[:], pool_scale.rearrange("(c p) -> p c", p=128), writes=["pscol"], group=("c", 1))
            T.dma("sp", cbcol[:], conv_b.rearrange("(c p) -> p c", p=128), writes=["cbcol"], group=("c", 2))
            for k in range(4):
                T.dma("sp", cwcol[:, k, :], conv_w[k].rearrange("(c p) -> p c", p=128), writes=["cwcol"],
                      group=("c", 3))
        T.dma("sp", dtb_b[:], dt_bias.partition_broadcast(128), writes=["dtb_b"], group=("c", 6))
        T.dma("sp", A_b[:], a_log.partition_broadcast(128), writes=["A_b"], group=("c", 7))
        T.dma("sp", D_b[:], d_skip.partition_broadcast(128), writes=["D_b"], group=("c", 8))
        T.op("dve", lambda: V.tensor_tensor(out=Ddiag[:], in0=ident_f[:].unsqueeze(1).to_broadcast([128, NH, 128]),
                                            in1=D_b[:].unsqueeze(2).to_broadcast([128, NH, 128]), op=ALU.mult),
             ["ident_f", "D_b"], ["Ddiag"])
        T.op("act", lambda: A_.activation(out=A_b[:], in_=A_b[:], func=AF.Exp), ["A_b"], ["A_b"])
        T.op("dve", lambda: V.tensor_scalar(out=A_b[:], in0=A_b[:], scalar1=-1.0, scalar2=None, op0=ALU.mult),
             ["A_b"], ["A_b"])

        xT = sb("xT", [128, KC, NTMAX])
        xn = sb("xn", [128, KC, NTMAX], BF16)
        wsl = sb("wsl", [128, NSLOT, SLOTW], BF16)
        rs = sb("rs", [128, 512])
        hT = sb("hT", [128, NG, 512])
        hTb = sb("hTb", [128, NG, 512], BF16)
        convhist = sb("convhist", [128, 24, 3])
        poolhist = sb("poolhist", [128, 8, 15])
        T.op("pool", lambda: P_.memset(hT[:], 0.0), [], [("hT", g) for g in range(NG)])
        T.op("pool", lambda: P_.memset(hTb[:], 0.0), [], [("hTb", g) for g in range(NG)])
        T.op("pool", lambda: P_.memset(convhist[:], 0.0), [], [("convhist", c) for c in range(24)])
        T.op("pool", lambda: P_.memset(poolhist[:], 0.0), [], [("poolhist", c) for c in range(8)])

        def wview(W, c0, n):
            return W[:, c0:c0 + n].rearrange("(kc p) n -> p kc n", p=128)

        def pass_specs():
            sp_ = []

            def ffn(w_i, w_o_):
                for jb in range(NJ // 2):
                    sp_.append(("ffn_in", [(wview(w_i, jb * 256, 256), 0, 8, 256),
                                           (wview(w_i, DFF + jb * 256, 256), 2048, 8, 256)]))
                for mb in range(4):
                    sp_.append(("ffn_out", [(wview(w_o_, mb * 256, 256), 0, NJ, 256)]))
            if cfg["ffn1"]:
                ffn(ffn1_w_in, ffn1_w_out)
            if cfg["mix"]:
                sp_.append(("dt", [(wview(w_in, OFF_DT, 32), 0, 8, 32)]))
                for g in range(NG):
                    sp_.append(("xa", [(wview(w_in, OFF_X + g * 512, 256), 0, 8, 256)]))
                    sp_.append(("xb", [(wview(w_in, OFF_X + g * 512 + 256, 256), 0, 8, 256)]))
                    sp_.append(("bc", [(wview(w_in, OFF_B + g * 128, 128), 0, 8, 128),
                                       (wview(w_in, OFF_C + g * 128, 128), 1024, 8, 128)]))
                    sp_.append(("z", [(wview(w_in, OFF_Z + g * 512, 512), 0, 8, 512)]))
                for ub in range(4):
                    sp_.append(("u", [(wview(w_in, OFF_U + ub * 256, 256), 0, 8, 256)]))
                for mb in range(4):
                    sp_.append(("g1", [(wview(w_in, OFF_G1 + mb * 256, 256), 0, 8, 256)]))
                    sp_.append(("sso", [(wview(ssd_w_out, mb * 256, 256), 0, 16, 256)]))
                    sp_.append(("g0", [(wview(w_in, OFF_G0 + mb * 256, 256), 0, 8, 256)]))
                    sp_.append(("pwo", [(wview(pool_w_out, mb * 256, 256), 0, 8, 256)]))
                for mb in range(4):
                    sp_.append(("wo", [(wview(w_o, mb * 256, 256), 0, 8, 256)]))
            if cfg["ffn2"]:
                ffn(ffn2_w_in, ffn2_w_out)
            return sp_

        specs = []
        for _ in PASSES:
            specs += pass_specs()
        wst = {"issued": 0, "consumed": 0}

        def w_issue(i):
            slot = i % NSLOT
            for pi, (src, off, k, n) in enumerate(specs[i][1]):
                dst = wsl[:, slot, off:off + k * n].rearrange("p (k n) -> p k n", k=k)
                T.dma("pool", dst, src, writes=[("w", slot, pi)], group=("w", slot, pi))

        def w_next(tag):
            i = wst["consumed"]
            assert specs[i][0] == tag, (specs[i][0], tag)
            while wst["issued"] < min(len(specs), i + NSLOT):
                w_issue(wst["issued"])
                wst["issued"] += 1
            wst["consumed"] += 1
            slot = i % NSLOT
            keys = [("w", slot, pi) for pi in range(len(specs[i][1]))]

            def view(off, k, n):
                return wsl[:, slot, off:off + k * n].rearrange("p (k n) -> p k n", k=k)
            return view, keys

        def mm_group(out, pairs, reads, wkey):
            n = len(pairs)
            for i, (l, r) in enumerate(pairs):
                T.op("pe", lambda l=l, r=r, i=i: nc.tensor.matmul(out, lhsT=l, rhs=r, start=(i == 0),
                                                                  stop=(i == n - 1)),
                     reads, [wkey], inc=(i == n - 1))

        def tr_group(outs_ins, ident, reads, wkey):
            n = len(outs_ins)
            for i, (o, a) in enumerate(outs_ins):
                T.op("pe", lambda o=o, a=a: nc.tensor.transpose(out=o, in_=a, identity=ident),
                     reads, [wkey], inc=(i == n - 1))

        def bc_mid(ap2, n):
            return ap2.unsqueeze(2).to_broadcast([128, ap2.shape[1], n])

        def bc_out(ap2, n):
            return ap2.unsqueeze(1).to_broadcast([128, n, ap2.shape[1]])

        def emit_pass(pi, tiles):
            nt = len(tiles)
            NT = nt * 128
            ptiles = [t for t in tiles if t < NPT]
            npt = len(ptiles)
            NTp = npt * 128
            has_s = STILE in tiles
            first_pass = (tiles[0] == 0)
            more_prompt = (ptiles[-1] < NPT - 1)
            pieces = []
            p0 = 0
            while p0 < NT:
                pn = min(512, NT - p0)
                pieces.append((p0, pn))
                p0 += pn

            def tkeys(name, p0, pn):
                return [(name, ti) for ti in range(p0 // 128, (p0 + pn) // 128)]

            with contextlib.ExitStack() as scope:
                xin = sb("xin", [128, 3, D], F32, scope)
                for ti, tile in enumerate(tiles):
                    src = xp[tile * 128:(tile + 1) * 128, :] if tile < NPT else xs[:, :]
                    slot = ti % 3
                    T.dma("sp", xin[:, slot, :], src, writes=[("xin", slot)], group=("xin", slot))
                    for half in range(2):
                        bank, bkey = nbf()
                        tr_group([(bank[:, c4 * 128:(c4 + 1) * 128],
                                   xin[:, slot, (half * 4 + c4) * 128:(half * 4 + c4 + 1) * 128])
                                  for c4 in range(4)], ident_f[:], [("xin", slot), "ident_f"], bkey)
                        T.op("act", lambda bank=bank, half=half, ti=ti: A_.copy(
                            out=xT[:, half * 4:(half + 1) * 4, ti * 128:(ti + 1) * 128],
                            in_=bank[:].rearrange("p (c t) -> p c t", c=4)), [bkey], [("xT", ti)])
                T.barrier()

            def rmsnorm(gi, scope):
                sq = sb("sq", [128, KC, 512], BF16, scope)
                for (p0, pn) in pieces:
                    xk = tkeys("xT", p0, pn)
                    T.op("pool", lambda p0=p0, pn=pn: P_.tensor_tensor(
                        out=sq[:, 0:4, 0:pn], in0=xT[:, 0:4, p0:p0 + pn], in1=xT[:, 0:4, p0:p0 + pn], op=ALU.mult),
                        xk, [("sq", 0)])
                    T.op("act", lambda p0=p0, pn=pn: A_.activation(
                        out=sq[:, 4:8, 0:pn], in_=xT[:, 4:8, p0:p0 + pn], func=AF.Square), xk, [("sq", 1)])
                    bank, bkey = nbf()
                    mm_group(bank[:, 0:pn], [(ones_b[:], sq[:, c, 0:pn]) for c in range(KC)],
                             [("sq", 0), ("sq", 1), "ones_b"], bkey)
                    T.op("act", lambda bank=bank, pn=pn: A_.activation(
                        out=rs[:, 0:pn], in_=bank[:, 0:pn], func=AF.Sqrt, scale=1.0 / D, bias=eps_col[:, 0:1]),
                        [bkey, "eps_col"], ["rs"])
                    T.op("dve", lambda pn=pn: V.reciprocal(out=rs[:, 0:pn], in_=rs[:, 0:pn]), ["rs"], ["rs"])
                    for c in range(KC):
                        T.op("dve", lambda c=c, p0=p0, pn=pn: V.scalar_tensor_tensor(
                            out=xn[:, c, p0:p0 + pn], in0=xT[:, c, p0:p0 + pn], scalar=gcols[:, gi, c:c + 1],
                            in1=rs[:, 0:pn], op0=ALU.mult, op1=ALU.mult),
                            xk + ["rs", "gcols"], tkeys("xn", p0, pn))

            def ffn(gi, scope):
                rmsnorm(gi, scope)
                act = sb("act", [128, NJ, NTMAX], BF16, scope)
                sg = sb("sg", [128, 2, 512], F32, scope)
                cnt = 0
                for jb in range(NJ // 2):
                    view, wk = w_next("ffn_in")
                    wg = view(0, 8, 256)
                    wu = view(2048, 8, 256)
                    for sub in range(2):
                        j = jb * 2 + sub
                        for pidx, (p0, pn) in enumerate(pieces):
                            xk = tkeys("xn", p0, pn)
                            bg, bgk = nbf()
                            bu, buk = nbf()
                            mm_group(bg[:, 0:pn], [(wg[:, kc, sub * 128:(sub + 1) * 128], xn[:, kc, p0:p0 + pn])
                                                   for kc in range(KC)], wk + xk, bgk)
                            mm_group(bu[:, 0:pn], [(wu[:, kc, sub * 128:(sub + 1) * 128], xn[:, kc, p0:p0 + pn])
                                                   for kc in range(KC)], wk + xk, buk)
                            s2 = cnt % 2
                            cnt += 1
                            T.op("act", lambda bg=bg, pn=pn, s2=s2: A_.activation(
                                out=sg[:, s2, 0:pn], in_=bg[:, 0:pn], func=AF.Silu), [bgk], [("sg", s2)])
                            T.op("dve", lambda bu=bu, pn=pn, p0=p0, j=j, s2=s2: V.tensor_tensor(
                                out=act[:, j, p0:p0 + pn], in0=bu[:, 0:pn], in1=sg[:, s2, 0:pn], op=ALU.mult),
                                [buk, ("sg", s2)], [("act", j, pidx)])
                for mb in range(4):
                    view, wk = w_next("ffn_out")
                    wo = view(0, NJ, 256)
                    for sub in range(2):
                        m = mb * 2 + sub
                        for pidx, (p0, pn) in enumerate(pieces):
                            bank, bkey = nbf()
                            mm_group(bank[:, 0:pn], [(wo[:, j, sub * 128:(sub + 1) * 128], act[:, j, p0:p0 + pn])
                                                     for j in range(NJ)],
                                     wk + [("act", j, pidx) for j in range(NJ)], bkey)
                            xk = tkeys("xT", p0, pn)
                            T.op("dve", lambda bank=bank, m=m, p0=p0, pn=pn: V.scalar_tensor_tensor(
                                out=xT[:, m, p0:p0 + pn], in0=bank[:, 0:pn], scalar=0.5, in1=xT[:, m, p0:p0 + pn],
                                op0=ALU.mult, op1=ALU.add), [bkey] + xk, xk)

            def mix():
                spec = [(ti, t) for ti, t in enumerate(tiles) if t == NPT - 1 or t == STILE]
                with contextlib.ExitStack() as ms:
                    with contextlib.ExitStack() as s0:
                        rmsnorm(1, s0)
                        T.barrier()
                    ynT = sb("ynT", [128, 16, NTMAX], BF16, ms)
                    stage = sb("stage", [128, 2, 256], F32, ms)
                    stcnt = [0]

                    def tokmajor_out(wblk, wk, ncols, kind, col0):
                        for (ti, t) in spec:
                            bank, bkey = nbf()
                            mm_group(bank[:, 0:ncols], [(xn[:, kc, ti * 128:(ti + 1) * 128], wblk[:, kc, :])
                                                        for kc in range(KC)], wk + [("xn", ti)], bkey)
                            s2 = stcnt[0] % 2
                            stcnt[0] += 1
                            T.op("act", lambda bank=bank, s2=s2: A_.copy(out=stage[:, s2, 0:ncols],
                                                                         in_=bank[:, 0:ncols]),
                                 [bkey], [("stage", s2)])
                            if kind == "conv":
                                if t == NPT - 1:
                                    T.dma("sp", o_conv_p[0:3, col0:col0 + ncols], stage[125:128, s2, 0:ncols],
                                          reads=[("stage", s2)], group=("stg", s2))
                                else:
                                    for r in range(3):
                                        T.dma("sp", o_conv_s[:, r, col0:col0 + ncols],
                                              stage[5 + r::8, s2, 0:ncols], reads=[("stage", s2)],
                                              group=("stg", s2))
                            else:
                                if t == NPT - 1:
                                    T.dma("sp", o_pool_p[0:15, col0:col0 + ncols], stage[113:128, s2, 0:ncols],
                                          reads=[("stage", s2)], group=("stg", s2))
                                else:
                                    T.dma("sp", o_pool_s[:, 7:15, col0:col0 + ncols], stage[:, s2, 0:ncols],
                                          reads=[("stage", s2)], group=("stg", s2))

                    with contextlib.ExitStack() as sa:
                        def sba(name, shape, dt=F32):
                            return sb(name, shape, dt, sa)
                        dtr = sba("dtr", [128, nt, NH])
                        dt_ = sba("dt_", [128, nt, NH])
                        dtA = sba("dtA", [128, nt, NH])
                        acs = sba("acs", [128, nt, NH])
                        nacs = sba("nacs", [128, nt, NH])
                        eacs = sba("eacs", [128, nt, NH])
                        dend = sba("dend", [128, nt, NH])
                        decb = sba("decb", [128, nt, NH])
                        hi = sba("hi", [128, nt, NH], BF16)
                        lo = sba("lo", [128, nt, NH], BF16)
                        x_tok = sba("x_tok", [128, nt, 512], BF16)
                        zt_all = sba("zt_all", [128, nt, 512], BF16)
                        BTg = sba("BTg", [128, NTMAX], BF16)
                        CTg = sba("CTg", [128, NTMAX], BF16)
                        Btok = sba("Btok", [128, nt, 128], BF16)
                        if has_s:
                            shist = sba("shist", [128, 24, NSEQ * 3])
                            with contextlib.ExitStack() as sst:
                                stc = sb("stc", [128, 2, 512], F32, sst)
                                for cb in range(6):
                                    s2 = cb % 2
                                    T.dma("sp", stc[0:48, s2, :],
                                          st_conv[:, :, cb * 512:(cb + 1) * 512].rearrange("s r c -> (s r) c"),
                                          writes=[("stc", s2)], group=("stc", s2))
                                    bank, bkey = nbf()
                                    tr_group([(bank[:, j * 48:(j + 1) * 48], stc[0:48, s2, j * 128:(j + 1) * 128])
                                              for j in range(4)], ident_f[0:48, 0:48], [("stc", s2), "ident_f"], bkey)
                                    T.op("act", lambda bank=bank, cb=cb: A_.copy(
                                        out=shist[:, cb * 4:(cb + 1) * 4, :],
                                        in_=bank[:, 0:192].rearrange("p (j q) -> p j q", j=4)), [bkey], ["shist"])
                                T.barrier()

                        view, wk = w_next("dt")
                        wdt = view(0, 8, 32)
                        bank, bkey = nbf()
                        for ti in range(nt):
                            mm_group(bank[:, ti * 32:(ti + 1) * 32],
                                     [(xn[:, kc, ti * 128:(ti + 1) * 128], wdt[:, kc, :]) for kc in range(KC)],
                                     wk + [("xn", ti)], bkey)
                        T.op("dve", lambda bank=bank: V.tensor_tensor(
                            out=dtr[:], in0=bank[:, 0:nt * 32].rearrange("p (t h) -> p t h", h=NH),
                            in1=bc_out(dtb_b[:], nt), op=ALU.add), [bkey, "dtb_b"], ["dtr"])
                        T.op("act", lambda: A_.activation(out=dtr[:], in_=dtr[:], func=AF.Exp), ["dtr"], ["dtr"])
                        T.op("act", lambda: A_.activation(out=dt_[:], in_=dtr[:], func=AF.Ln, bias=1.0, scale=1.0),
                             ["dtr"], ["dt_"])
                        T.op("dve", lambda: V.tensor_tensor(out=dtA[:], in0=dt_[:], in1=bc_out(A_b[:], nt),
                                                            op=ALU.mult), ["dt_", "A_b"], ["dtA"])
                        T.op("act", lambda: A_.copy(out=hi[:], in_=dtA[:]), ["dtA"], ["hi"])
                        T.op("dve", lambda: V.tensor_tensor(out=lo[:], in0=dtA[:], in1=hi[:], op=ALU.subtract),
                             ["dtA", "hi"], ["lo"])
                        for ti, t in enumerate(tiles):
                            kind = 1 if t == STILE else 0
                            bank, bkey = nbf()
                            mm_group(bank[:, 0:32], [(tri_f[:, kind, :], dtA[:, ti, :])], ["tri_f", "dtA"], bkey)
                            mm_group(bank[:, 32:64], [((blk_f[:] if kind else ones_f[:]), dtA[:, ti, :])],
                                     ["blk_f", "ones_f", "dtA"], bkey)
                            T.op("act", lambda bank=bank, ti=ti: A_.copy(out=acs[:, ti, :], in_=bank[:, 0:32]),
                                 [bkey], ["acs"])
                            T.op("act", lambda bank=bank, ti=ti: A_.mul(out=nacs[:, ti, :], in_=bank[:, 0:32],
                                                                        mul=-1.0), [bkey], ["nacs"])
                            T.op("act", lambda bank=bank, ti=ti: A_.activation(out=eacs[:, ti, :], in_=bank[:, 0:32],
                                                                               func=AF.Exp), [bkey], ["eacs"])
                            T.op("act", lambda bank=bank, ti=ti: A_.activation(out=decb[:, ti, :],
                                                                               in_=bank[:, 32:64], func=AF.Exp),
                                 [bkey], ["decb"])
                            T.op("dve", lambda bank=bank, ti=ti: V.tensor_tensor(
                                out=dend[:, ti, :], in0=bank[:, 32:64], in1=acs[:, ti, :], op=ALU.subtract),
                                [bkey, "acs"], ["dend"])
                            T.op("act", lambda ti=ti: A_.activation(out=dend[:, ti, :], in_=dend[:, ti, :],
                                                                    func=AF.Exp), ["dend"], ["dend"])

                        for g in range(NG):
                            hs0 = 8 * g
                            with contextlib.ExitStack() as sc:
                                xpre = sb("xpre", [128, 2, 3 + NTMAX], F32, sc)
                                accb = sb("accb", [128, 2, NTMAX], F32, sc)
                                xc = sb("xc", [128, 2, NTMAX], BF16, sc)
                                if has_s:
                                    xpre_s = sb("xpre_s", [128, 2, NSEQ, 3 + TS], F32, sc)
                                ccnt = [0]

                                def conv_chunk(cc, wsl_, wk, dest, dkeys):
                                    s2 = ccnt[0] % 2
                                    ccnt[0] += 1
                                    kx, ka = ("xpre", s2), ("accb", s2)
                                    for (p0, pn) in pieces:
                                        bank, bkey = nbf()
                                        mm_group(bank[:, 0:pn], [(wsl_[:, kc, :], xn[:, kc, p0:p0 + pn])
                                                                 for kc in range(KC)], wk + tkeys("xn", p0, pn), bkey)
                                        if p0 < NTp:
                                            T.op("act", lambda bank=bank, p0=p0, pn=pn: A_.copy(
                                                out=xpre[:, s2, 3 + p0:3 + p0 + pn], in_=bank[:, 0:pn]), [bkey], [kx])
                                        else:
                                            T.op("act", lambda bank=bank: A_.copy(
                                                out=xpre_s[:, s2, :, 3:3 + TS],
                                                in_=bank[:, 0:128].rearrange("p (s t) -> p s t", t=TS)), [bkey], [kx])
                                        T.op("act", lambda bank=bank, p0=p0, pn=pn: A_.activation(
                                            out=accb[:, s2, p0:p0 + pn], in_=bank[:, 0:pn], func=AF.Identity,
                                            scale=cwcol[:, 3, cc:cc + 1], bias=cbcol[:, cc:cc + 1]),
                                            [bkey, "cwcol", "cbcol"], [ka])
                                    if npt:
                                        T.op("pool", lambda: P_.tensor_copy(out=xpre[:, s2, 0:3], in_=convhist[:, cc, :]),
                                             [("convhist", cc)], [kx])
                                        for k in range(3):
                                            T.op("dve", lambda k=k: V.scalar_tensor_tensor(
                                                out=accb[:, s2, 0:NTp], in0=xpre[:, s2, k:k + NTp],
                                                scalar=cwcol[:, k, cc:cc + 1], in1=accb[:, s2, 0:NTp],
                                                op0=ALU.mult, op1=ALU.add), [kx, ka, "cwcol"], [ka])
                                        if more_prompt:
                                            T.op("pool", lambda: P_.tensor_copy(out=convhist[:, cc, :],
                                                                               in_=xpre[:, s2, NTp:NTp + 3]),
                                                 [kx], [("convhist", cc)])
                                    if has_s:
                                        T.op("pool", lambda: P_.tensor_copy(
                                            out=xpre_s[:, s2, :, 0:3],
                                            in_=shist[:, cc, :].rearrange("p (s r) -> p s r", r=3)), ["shist"], [kx])
                                        av = accb[:, s2, NTp:NT].rearrange("p (s t) -> p s t", t=TS)
                                        for k in range(3):
                                            T.op("dve", lambda k=k: V.scalar_tensor_tensor(
                                                out=av, in0=xpre_s[:, s2, :, k:k + TS],
                                                scalar=cwcol[:, k, cc:cc + 1], in1=av,
                                                op0=ALU.mult, op1=ALU.add), [kx, ka, "cwcol"], [ka])
                                    T.op("act", lambda: A_.activation(out=dest, in_=accb[:, s2, 0:NT], func=AF.Silu),
                                         [ka], dkeys)

                                def to_tokmajor(srcT, skeys, dst3, dkeys):
                                    bb, bbk = nbb()
                                    tr_group([(bb[:, ti * 128:(ti + 1) * 128], srcT[:, ti * 128:(ti + 1) * 128])
                                              for ti in range(nt)], ident_b[:], skeys + ["ident_b"], bbk)
                                    T.op("dve", lambda: V.tensor_copy(
                                        out=dst3, in_=bb[:, 0:NT].rearrange("p (t c) -> p t c", c=128)),
                                        [bbk], dkeys)

                                xcc = 0
                                pend = []

                                def flush():
                                    while pend:
                                        pend.pop(0)()
                                for ab, tag in enumerate(("xa", "xb")):
                                    view, wk = w_next(tag)
                                    wb = view(0, 8, 256)
                                    for sub in range(2):
                                        j = ab * 2 + sub
                                        cc = 4 * g + j
                                        s2 = xcc % 2
                                        xcc += 1
                                        conv_chunk(cc, wb[:, :, sub * 128:(sub + 1) * 128], wk, xc[:, s2, 0:NT],
                                                   [("xc", s2)])
                                        flush()
                                        pend.append(lambda s2=s2, j=j: to_tokmajor(
                                            xc[:, s2, :], [("xc", s2)], x_tok[:, :, j * 128:(j + 1) * 128], ["x_tok"]))
                                    tokmajor_out(wb, wk, 256, "conv", g * 512 + ab * 256)
                                view, wk = w_next("bc")
                                wB = view(0, 8, 128)
                                wC = view(1024, 8, 128)
                                conv_chunk(16 + g, wB, wk, BTg[:, 0:NT], ["BTg"])
                                flush()
                                conv_chunk(20 + g, wC, wk, CTg[:, 0:NT], ["CTg"])
                                to_tokmajor(BTg, ["BTg"], Btok[:, :, :], ["Btok"])
                                tokmajor_out(wB, wk, 128, "conv", 2048 + g * 128)
                                tokmajor_out(wC, wk, 128, "conv", 2560 + g * 128)
                                view, wkz = w_next("z")
                                wz = view(0, 8, 512)
                                ztmp = sb("ztmp", [128, 2, 512], F32, sc)
                                for ti in range(nt):
                                    q2 = ti % 2
                                    bz, bzk = nbf()
                                    mm_group(bz[:], [(xn[:, kc, ti * 128:(ti + 1) * 128], wz[:, kc, :]) for kc in range(KC)],
                                             wkz + [("xn", ti)], bzk)
                                    T.op("act", lambda bz=bz, q2=q2: A_.activation(
                                        out=ztmp[:, q2, :], in_=bz[:], func=AF.Tanh, scale=0.5), [bzk], [("ztmp", q2)])
                                    T.op("dve", lambda bz=bz, q2=q2, ti=ti: V.scalar_tensor_tensor(
                                        out=zt_all[:, ti, :], in0=ztmp[:, q2, :], scalar=1.0, in1=bz[:], op0=ALU.add,
                                        op1=ALU.mult), [("ztmp", q2), bzk], [("zt", ti)])
                                T.barrier()

                            with contextlib.ExitStack() as sd:
                                def sbd(name, shape, dt=F32):
                                    return sb(name, shape, dt, sd)
                                xdt = sbd("xdt", [128, 3, 512], BF16)
                                T1 = sbd("T1", [128, 2, 512])
                                cbT = sbd("cbT", [128, 2, 128], BF16)
                                decT = sbd("decT", [128, 2, 4, 128], BF16)
                                MT = sbd("MT", [128, 2, 4, 128], BF16)
                                ygs = sbd("ygs", [128, nt, 512], BF16)
                                ssg = sbd("ssg", [128, nt])
                                junk = sbd("junk", [128, 512], BF16)
                                xdd = sbd("xdd", [128, 2, 512], BF16)
                                yn = sbd("yn", [128, 2, 512], BF16)
                                hout = sbd("hout", [128, 4, 128])
                                snorm_g = sbd("snorm_g", [128, 512])
                                T.dma("sp", snorm_g[:], ssd_norm[g * 512:(g + 1) * 512].partition_broadcast(128),
                                      writes=["snorm_g"], group=("sng", 0))
                                NH0 = 6
                                if has_s:
                                    CTm = sbd("CTm", [128, NSEQ, 128], BF16)
                                    h0 = sbd("h0", [128, NH0, 4, 128])
                                    h0T = sbd("h0T", [128, 2, 512], BF16)
                                    Btm = sbd("Btm", [128, 2, 128], BF16)
                                    dtAx = sbd("dtAx", [128, 512])
                                    dcol = sbd("dcol", [128, 4, NSEQ])
                                hc = [0]
                                bys = {}

                                def h3(ap):
                                    return ap.rearrange("p (h q) -> p h q", q=HP)

                                def ph0(ti, t):
                                    kind = 1 if t == STILE else 0
                                    cols = slice(ti * 128, (ti + 1) * 128)
                                    s3, s2 = ti % 3, ti % 2
                                    bcb, bcbk = nbf()
                                    mm_group(bcb[:, 0:128], [(BTg[:, cols], CTg[:, cols])], ["BTg", "CTg"], bcbk)
                                    T.op("act", lambda: A_.copy(out=cbT[:, s2, :], in_=bcb[:, 0:128]),
                                         [bcbk], [("cbT", s2)])
                                    xt3 = h3(x_tok[:, ti, :])
                                    T.op("pool", lambda: P_.tensor_tensor(
                                        out=h3(xdt[:, s3, :]), in0=xt3, in1=bc_mid(dt_[:, ti, hs0:hs0 + 8], HP),
                                        op=ALU.mult), ["x_tok", "dt_"], [("xdt", s3)])

                                def ph1(ti, t):
                                    kind = 1 if t == STILE else 0
                                    s3, s2 = ti % 3, ti % 2
                                    by, byk = nbf(hold=True)
                                    bys[ti] = (by, byk)
                                    brs = []
                                    for half in range(2):
                                        h4 = 4 * half
                                        br, brk = nbf(hold=True)
                                        brs.append((br, brk))
                                        rd = ["ones_b", "ident_b", "negm", "hi", "lo", "tri_b"]
                                        T.op("pe", lambda br=br: nc.tensor.matmul(
                                            br[:], lhsT=ident_b[:], rhs=negm[:, kind].rearrange("p a b -> p (a b)"),
                                            start=True, stop=False), rd, [brk], inc=False)
                                        for hh in range(4):
                                            hg = hs0 + h4 + hh
                                            T.op("pe", lambda br=br, hh=hh, hg=hg: nc.tensor.matmul(
                                                br[:, hh * 128:(hh + 1) * 128],
                                                lhsT=hi[:, ti, hg:hg + 1].to_broadcast([128, 128]),
                                                rhs=tri_b[:, kind, :], start=False, stop=False), rd, [brk], inc=False)
                                            T.op("pe", lambda br=br, hh=hh, hg=hg: nc.tensor.matmul(
                                                br[:, hh * 128:(hh + 1) * 128],
                                                lhsT=lo[:, ti, hg:hg + 1].to_broadcast([128, 128]),
                                                rhs=tri_b[:, kind, :], start=False, stop=(hh == 3)), rd, [brk],
                                                inc=(hh == 3))
                                    for half in range(2):
                                        h4 = 4 * half
                                        br, brk = brs[half]
                                        for hh in range(4):
                                            hg = hs0 + h4 + hh
                                            T.op("act", lambda br=br, hh=hh, hg=hg, half=half: A_.activation(
                                                out=decT[:, half, hh, :], in_=br[:, hh * 128:(hh + 1) * 128], func=AF.Exp,
                                                bias=nacs[:, ti, hg:hg + 1], scale=1.0), [brk, "nacs"], [("decT", half)])
                                        release(brk)
                                        T.op("dve", lambda half=half: V.tensor_tensor(
                                            out=MT[:, half], in0=decT[:, half], in1=bc_out(cbT[:, s2, :], 4), op=ALU.mult),
                                            [("decT", half), ("cbT", s2)], [("MT", half)])
                                    for half in range(2):
                                        h4 = 4 * half
                                        for hh in range(4):
                                            hl = h4 + hh
                                            T.op("pe", lambda hh=hh, hl=hl, half=half: nc.tensor.matmul(
                                                by[:, hl * 64:(hl + 1) * 64], lhsT=MT[:, half, hh, :],
                                                rhs=xdt[:, s3, hl * 64:(hl + 1) * 64], start=True, stop=False),
                                                [("MT", half), ("xdt", s3)], [byk], inc=False)
                                            T.op("pe", lambda hl=hl: nc.tensor.matmul(
                                                by[:, hl * 64:(hl + 1) * 64], lhsT=Ddiag[:, hs0 + hl, :],
                                                rhs=x_tok[:, ti, hl * 64:(hl + 1) * 64], start=False, stop=True),
                                                ["Ddiag", "x_tok"], [byk], inc=True)

                                def ph2(ti, t):
                                    kind = 1 if t == STILE else 0
                                    cols = slice(ti * 128, (ti + 1) * 128)
                                    s3, s2 = ti % 3, ti % 2
                                    kxdt = ("xdt", s3)
                                    kT1 = ("T1", s2)
                                    kxdd = ("xdd", s2)
                                    T.op("pool", lambda: P_.tensor_tensor(
                                        out=h3(xdd[:, s2, :]), in0=h3(xdt[:, s3, :]),
                                        in1=bc_mid(dend[:, ti, hs0:hs0 + 8], HP), op=ALU.mult),
                                        [kxdt, "dend"], [kxdd])
                                    if kind == 0:
                                        ba, bak = nbf()
                                        mm_group(ba[:], [(CTg[:, cols], hTb[:, g, :])], ["CTg", ("hTb", g)], bak)
                                    else:
                                        ba, bak = nbf(hold=True)

                                        def ld(s):
                                            sl = s % NH0
                                            T.dma("sp", h0[:, sl].rearrange("p j n -> p (j n)"),
                                                  st_ssm[s, g * 512:(g + 1) * 512, :].rearrange("(p j) n -> p (j n)", j=4),
                                                  writes=[("h0", sl)], group=("h0", sl))

                                        def sA(s):
                                            sl = s % NH0
                                            q2 = s % 2
                                            bt, btk = nbf()
                                            tr_group([(bt[:, j * 128:(j + 1) * 128], h0[:, sl, j, :]) for j in range(4)],
                                                     ident_f[:], [("h0", sl), "ident_f"], btk)
                                            T.op("act", lambda bt=bt, q2=q2: A_.copy(
                                                out=h0T[:, q2, :].rearrange("n (p j) -> n j p", j=4),
                                                in_=bt[:].rearrange("n (j p) -> n j p", j=4)), [btk], [("h0T", q2)])
                                            T.op("act", lambda s=s, q2=q2: A_.activation(
                                                out=Btm[:, q2, :], in_=Btok[:, ti, :], func=AF.Copy,
                                                scale=rowmask[:, s:s + 1]), ["Btok", "rowmask"], [("Btm", q2)])

                                        def sB(s):
                                            sl = s % NH0
                                            q2 = s % 2
                                            kh = ("h0", sl)
                                            T.op("pe", lambda s=s, q2=q2: nc.tensor.matmul(
                                                ba[:], lhsT=CTm[:, s, :], rhs=h0T[:, q2, :], start=(s == 0),
                                                stop=(s == NSEQ - 1)), ["CTm", ("h0T", q2)], [bak])
                                            bs_, bsk = nbf()
                                            for j in range(4):
                                                mm_group(bs_[:, j * 128:(j + 1) * 128],
                                                         [(xdd[:, s2, j:512:4], Btm[:, q2, :])],
                                                         [kxdd, ("Btm", q2)], bsk)
                                            T.op("dve", lambda s=s, sl=sl, bs_=bs_: V.scalar_tensor_tensor(
                                                out=h0[:, sl].rearrange("p j n -> p (j n)"),
                                                in0=h0[:, sl].rearrange("p j n -> p (j n)"), scalar=dcol[:, 0, s:s + 1],
                                                in1=bs_[:], op0=ALU.mult, op1=ALU.add), [kh, "dcol", bsk], [kh])
                                            T.dma("sp", o_ssm_s[s, g * 512:(g + 1) * 512, :].rearrange(
                                                "(p j) n -> p (j n)", j=4), h0[:, sl].rearrange("p j n -> p (j n)"),
                                                reads=[kh], group=("h0o", sl))

                                        for s in range(4):
                                            ld(s)
                                        sA(0)
                                        for s in range(NSEQ):
                                            if s + 4 < NSEQ:
                                                ld(s + 4)
                                            if s + 1 < NSEQ:
                                                sA(s + 1)
                                            sB(s)
                                    T.op("dve", lambda: V.tensor_tensor(
                                        out=h3(T1[:, s2, :]), in0=h3(ba[:]),
                                        in1=bc_mid(eacs[:, ti, hs0:hs0 + 8], HP), op=ALU.mult),
                                        [bak, "eacs"], [kT1])
                                    if kind == 1:
                                        release(bak)
                                    by, byk = bys.pop(ti)
                                    T.op("dve", lambda: V.tensor_tensor(out=T1[:, s2, :], in0=by[:], in1=T1[:, s2, :],
                                                                        op=ALU.add), [byk, kT1], [kT1])
                                    release(byk)
                                    T.op("dve", lambda: V.tensor_tensor(out=T1[:, s2, :], in0=T1[:, s2, :],
                                                                        in1=zt_all[:, ti, :], op=ALU.mult),
                                         [kT1, ("zt", ti)], [kT1])
                                    T.op("dve", lambda: V.scalar_tensor_tensor(
                                        out=junk[:], in0=T1[:, s2, :], scalar=1.0, in1=T1[:, s2, :], op0=ALU.mult,
                                        op1=ALU.mult, accum_out=ssg[:, ti:ti + 1]), [kT1], ["junk", "ssg"])
                                    T.op("pool", lambda: P_.tensor_copy(out=ygs[:, ti, :], in_=T1[:, s2, :]),
                                         [kT1], ["ygs"])
                                    if kind == 0:
                                        bs_, bsk = nbf()
                                        mm_group(bs_[:], [(Btok[:, ti, :], xdd[:, s2, :])], ["Btok", kxdd], bsk)
                                        hv = h3(hT[:, g, :])
                                        T.op("dve", lambda: V.tensor_tensor(
                                            out=hv, in0=hv, in1=bc_mid(decb[:, ti, hs0:hs0 + 8], HP), op=ALU.mult),
                                            [("hT", g), "decb"], [("hT", g)])
                                        T.op("dve", lambda: V.tensor_tensor(
                                            out=hT[:, g, :], in0=hT[:, g, :], in1=bs_[:], op=ALU.add),
                                            [("hT", g), bsk], [("hT", g)])
                                        T.op("pool", lambda: P_.tensor_copy(out=hTb[:, g, :], in_=hT[:, g, :]),
                                             [("hT", g)], [("hTb", g)])
                                        if t == NPT - 1:
                                            bt, btk = nbf()
                                            tr_group([(bt[:, j * 128:(j + 1) * 128], hT[:, g, j * 128:(j + 1) * 128])
                                                      for j in range(4)], ident_f[:], [("hT", g), "ident_f"], btk)
                                            T.op("act", lambda: A_.copy(
                                                out=hout[:], in_=bt[:].rearrange("p (j n) -> p j n", n=128)),
                                                [btk], ["hout"])
                                            T.dma("sp", o_ssm_p[g * 512:(g + 1) * 512, :].rearrange(
                                                "(j p) n -> p j n", p=128), hout[:], reads=["hout"], group=("hout", 0))

                                if has_s:
                                    def sample_setup():
                                        ti = nt - 1
                                        cols = slice(ti * 128, (ti + 1) * 128)
                                        T.op("pool", lambda: P_.tensor_copy(
                                                out=h3(dtAx[:]), in_=bc_mid(dtA[:, ti, hs0:hs0 + 8], HP)), ["dtA"], ["dtAx"])
                                        bd, bdk = nbf()
                                        mm_group(bd[:, 0:NSEQ], [(dtAx[:, 0:512:4], rowmask[:, :])], ["dtAx", "rowmask"], bdk)
                                        T.op("act", lambda: A_.activation(out=dcol[:, 0, :], in_=bd[:, 0:NSEQ], func=AF.Exp),
                                                 [bdk], ["dcol"])
                                        T.op("pool", lambda: P_.tensor_tensor(
                                                out=CTm[:], in0=bc_out(CTg[:, cols], NSEQ), in1=smask[:], op=ALU.mult),
                                                ["CTg", "smask"], ["CTm"])
                                    sample_setup()
                                for step in range(nt + 2):
                                    if step < nt:
                                        ph0(step, tiles[step])
                                    if 0 <= step - 2 < nt:
                                        ph2(step - 2, tiles[step - 2])
                                    if 0 <= step - 1 < nt:
                                        ph1(step - 1, tiles[step - 1])
                                T.op("act", lambda: A_.activation(out=ssg[:], in_=ssg[:], func=AF.Sqrt,
                                                                  scale=1.0 / 512, bias=eps_col[:, 1:2]),
                                     ["ssg", "eps_col"], ["ssg"])
                                T.op("dve", lambda: V.reciprocal(out=ssg[:], in_=ssg[:]), ["ssg"], ["ssg"])
                                for ti in range(nt):
                                    s2 = ti % 2
                                    T.op("dve", lambda ti=ti, s2=s2: V.scalar_tensor_tensor(
                                        out=yn[:, s2, :], in0=ygs[:, ti, :], scalar=ssg[:, ti:ti + 1], in1=snorm_g[:],
                                        op0=ALU.mult, op1=ALU.mult), ["ygs", "ssg", "snorm_g"], [("yn", s2)])
                                    bb, bbk = nbb()
                                    tr_group([(bb[:, j * 128:(j + 1) * 128], yn[:, s2, j * 128:(j + 1) * 128])
                                              for j in range(4)], ident_b[:], [("yn", s2), "ident_b"], bbk)
                                    T.op("act", lambda bb=bb, ti=ti: A_.copy(
                                        out=ynT[:, 4 * g:4 * g + 4, ti * 128:(ti + 1) * 128],
                                        in_=bb[:, 0:512].rearrange("p (j t) -> p j t", t=128)), [bbk], [("ynT", ti)])
                                T.barrier()
                        T.barrier()

                    with contextlib.ExitStack() as sbs:
                        yp = sb("yp", [128, 8, NTMAX], BF16, sbs)
                        mrg = sb("mrg", [128, 8, NTMAX], BF16, sbs)
                        with contextlib.ExitStack() as sp1:
                            uext = sb("uext", [128, 2, 15 + NTMAX], F32, sp1)
                            swa = sb("swa", [128, 15 + NTMAX], F32, sp1)
                            swb = sb("swb", [128, 15 + NTMAX], F32, sp1)
                            dg = sb("dg", [128, 2, NTMAX], BF16, sp1)
                            fixb = sb("fixb", [128, 16], F32, sp1)
                            wgrp_t = sb("wgrp_t", [128, 8, 256], BF16, sp1)
                            T.dma("pool", wgrp_t[:], pool_w_group.rearrange("g (kc p) n -> p (g kc) n", p=128),
                                  writes=["wgrp_t"], group=("wgrp", 0))
                            if has_s:
                                sphist = sb("sphist", [128, 8, NSEQ, 15], F32, sp1)
                                stp = sb("stp", [128, 2, 512], F32, sp1)
                                uexs = sb("uexs", [128, 2, NSEQ, 15 + TS], F32, sp1)
                                swsa = sb("swsa", [128, NSEQ, 15 + TS], F32, sp1)
                                swsb = sb("swsb", [128, NSEQ, 15 + TS], F32, sp1)
                                cnt2 = 0
                                for hh in range(2):
                                    for cb in range(2):
                                        s2 = cnt2 % 2
                                        cnt2 += 1
                                        T.dma("sp", stp[0:120, s2, :],
                                              st_pool[hh * 8:(hh + 1) * 8, :, cb * 512:(cb + 1) * 512].rearrange(
                                                  "s r c -> (s r) c"), writes=[("stp", s2)], group=("stp", s2))
                                        bank, bkey = nbf()
                                        tr_group([(bank[:, j * 120:(j + 1) * 120], stp[0:120, s2, j * 128:(j + 1) * 128])
                                                  for j in range(4)], ident_f[0:120, 0:120], [("stp", s2), "ident_f"],
                                                 bkey)
                                        T.op("act", lambda bank=bank, hh=hh, cb=cb: A_.copy(
                                            out=sphist[:, cb * 4:(cb + 1) * 4, hh * 8:(hh + 1) * 8, :].rearrange(
                                                "p j s r -> p j (s r)"),
                                            in_=bank[:, 0:480].rearrange("p (j q) -> p j q", j=4)), [bkey], ["sphist"])
                                T.dma("sp", o_pool_s[:, 0:7, :], st_pool[:, 8:15, :], group=("pcp", 0))
                            wgrp = wgrp_t
                            wkg = ["wgrp_t"]
                            ucnt = 0
                            for ub in range(4):
                                win = 2 << ub
                                view, wk = w_next("u")
                                wu_ = view(0, 8, 256)
                                for sub in range(2):
                                    c = ub * 2 + sub
                                    s2 = ucnt % 2
                                    ucnt += 1
                                    ku = ("uext", s2)
                                    for (p0, pn) in pieces:
                                        bank, bkey = nbf()
                                        mm_group(bank[:, 0:pn], [(wu_[:, kc, sub * 128:(sub + 1) * 128],
                                                                  xn[:, kc, p0:p0 + pn]) for kc in range(KC)],
                                                 wk + tkeys("xn", p0, pn), bkey)
                                        if p0 < NTp:
                                            T.op("act", lambda bank=bank, p0=p0, pn=pn, s2=s2: A_.copy(
                                                out=uext[:, s2, 15 + p0:15 + p0 + pn], in_=bank[:, 0:pn]), [bkey], [ku])
                                        else:
                                            T.op("act", lambda bank=bank, s2=s2: A_.copy(
                                                out=uexs[:, s2, :, 15:15 + TS],
                                                in_=bank[:, 0:128].rearrange("p (s t) -> p s t", t=TS)), [bkey], [ku])
                                    if npt:
                                        T.op("pool", lambda s2=s2, c=c: P_.tensor_copy(out=uext[:, s2, 0:15],
                                                                                      in_=poolhist[:, c, :]),
                                             [("poolhist", c)], [ku])
                                        W_ = 15 + NTp
                                        src = uext[:, s2, 0:W_]
                                        bufs = [swa, swb]
                                        sh = 1
                                        bi = 0
                                        srck = ku
                                        while sh < win:
                                            dst = bufs[bi][:, 0:W_]
                                            T.op("dve", lambda dst=dst, src=src, sh=sh: V.tensor_tensor(
                                                out=dst[:, sh:W_], in0=src[:, sh:W_], in1=src[:, 0:W_ - sh], op=ALU.add),
                                                [srck], [("sw", bi)])
                                            src = dst
                                            srck = ("sw", bi)
                                            bi ^= 1
                                            sh *= 2
                                        T.op("dve", lambda src=src, s2=s2, sub=sub: V.scalar_tensor_tensor(
                                            out=dg[:, sub, 0:NTp], in0=src[:, 15:15 + NTp], scalar=1.0 / win,
                                            in1=uext[:, s2, 15:15 + NTp], op0=ALU.mult, op1=ALU.subtract),
                                            [srck, ku], [("dg", sub)])
                                        if first_pass:
                                            nfix = win - 1
                                            T.op("dve", lambda src=src, nfix=nfix: V.tensor_tensor(
                                                out=fixb[:, 0:nfix], in0=src[:, 15:15 + nfix], in1=invc[:, 0:nfix],
                                                op=ALU.mult), [srck, "invc"], ["fixb"])
                                            T.op("dve", lambda s2=s2, sub=sub, nfix=nfix: V.tensor_tensor(
                                                out=dg[:, sub, 0:nfix], in0=fixb[:, 0:nfix],
                                                in1=uext[:, s2, 15:15 + nfix], op=ALU.subtract),
                                                ["fixb", ku], [("dg", sub)])
                                        if more_prompt:
                                            T.op("pool", lambda s2=s2, c=c: P_.tensor_copy(
                                                out=poolhist[:, c, :], in_=uext[:, s2, NTp:NTp + 15]),
                                                [ku], [("poolhist", c)])
                                    if has_s:
                                        T.op("pool", lambda s2=s2, c=c: P_.tensor_copy(
                                            out=uexs[:, s2, :, 0:15], in_=sphist[:, c, :, :]), ["sphist"], [ku])
                                        W_ = 15 + TS
                                        src = uexs[:, s2]
                                        bufs = [swsa, swsb]
                                        sh = 1
                                        bi = 0
                                        srck = ku
                                        while sh < win:
                                            dst = bufs[bi]
                                            T.op("pool", lambda dst=dst, src=src, sh=sh: P_.tensor_tensor(
                                                out=dst[:, :, sh:W_], in0=src[:, :, sh:W_], in1=src[:, :, 0:W_ - sh],
                                                op=ALU.add), [srck], [("sws", bi)])
                                            src = dst[:]
                                            srck = ("sws", bi)
                                            bi ^= 1
                                            sh *= 2
                                        T.op("dve", lambda src=src, s2=s2, sub=sub: V.scalar_tensor_tensor(
                                            out=dg[:, sub, NTp:NT].rearrange("p (s t) -> p s t", t=TS),
                                            in0=src[:, :, 15:15 + TS], scalar=1.0 / win,
                                            in1=uexs[:, s2, :, 15:15 + TS], op0=ALU.mult, op1=ALU.subtract),
                                            [srck, ku], [("dg", sub)])
                                tokmajor_out(wu_, wk, 256, "pool", ub * 256)
                                for m2 in range(2):
                                    m = ub * 2 + m2
                                    for (p0, pn) in pieces:
                                        bank, bkey = nbf()
                                        mm_group(bank[:, 0:pn], [(wgrp[:, ub * 2 + k2, m2 * 128:(m2 + 1) * 128],
                                                                  dg[:, k2, p0:p0 + pn]) for k2 in range(2)],
                                                 wkg + [("dg", 0), ("dg", 1)], bkey)
                                        T.op("act", lambda bank=bank, m=m, p0=p0, pn=pn: A_.activation(
                                            out=yp[:, m, p0:p0 + pn], in_=bank[:, 0:pn], func=AF.Copy,
                                            scale=pscol[:, m:m + 1]), [bkey, "pscol"], [("yp", m)])
                            T.barrier()
                        with contextlib.ExitStack() as sp2:
                            gs = sb("gs", [128, 2, NTMAX], F32, sp2)
                            m1 = sb("m1", [128, 2, NTMAX], F32, sp2)

                            def gate(tag):
                                view, wk = w_next(tag)
                                wgt = view(0, 8, 256)
                                for sub in range(2):
                                    for (p0, pn) in pieces:
                                        bank, bkey = nbf()
                                        mm_group(bank[:, 0:pn], [(wgt[:, kc, sub * 128:(sub + 1) * 128],
                                                                  xn[:, kc, p0:p0 + pn]) for kc in range(KC)],
                                                 wk + tkeys("xn", p0, pn), bkey)
                                        T.op("act", lambda bank=bank, sub=sub, p0=p0, pn=pn: A_.activation(
                                            out=gs[:, sub, p0:p0 + pn], in_=bank[:, 0:pn], func=AF.Sigmoid),
                                            [bkey], [("gs", sub)])
                            for mb in range(4):
                                gate("g1")
                                view, wk = w_next("sso")
                                wso = view(0, 16, 256)
                                for sub in range(2):
                                    for (p0, pn) in pieces:
                                        bank, bkey = nbf()
                                        mm_group(bank[:, 0:pn], [(wso[:, cc, sub * 128:(sub + 1) * 128],
                                                                  ynT[:, cc, p0:p0 + pn]) for cc in range(16)],
                                                 wk + tkeys("ynT", p0, pn), bkey)
                                        T.op("dve", lambda bank=bank, sub=sub, p0=p0, pn=pn: V.tensor_tensor(
                                            out=m1[:, sub, p0:p0 + pn], in0=bank[:, 0:pn], in1=gs[:, sub, p0:p0 + pn],
                                            op=ALU.mult), [bkey, ("gs", sub)], [("m1", sub)])
                                gate("g0")
                                view, wk = w_next("pwo")
                                wpo = view(0, 8, 256)
                                for sub in range(2):
                                    m = mb * 2 + sub
                                    for (p0, pn) in pieces:
                                        bank, bkey = nbf()
                                        mm_group(bank[:, 0:pn], [(wpo[:, kc, sub * 128:(sub + 1) * 128],
                                                                  yp[:, kc, p0:p0 + pn]) for kc in range(KC)],
                                                 wk + [("yp", kc) for kc in range(KC)], bkey)
                                        T.op("dve", lambda bank=bank, sub=sub, p0=p0, pn=pn: V.tensor_tensor(
                                            out=gs[:, sub, p0:p0 + pn], in0=bank[:, 0:pn], in1=gs[:, sub, p0:p0 + pn],
                                            op=ALU.mult), [bkey, ("gs", sub)], [("gs", sub)])
                                        T.op("pool", lambda sub=sub, m=m, p0=p0, pn=pn: P_.tensor_tensor(
                                            out=mrg[:, m, p0:p0 + pn], in0=gs[:, sub, p0:p0 + pn],
                                            in1=m1[:, sub, p0:p0 + pn], op=ALU.add),
                                            [("gs", sub), ("m1", sub)], [("mrg", m)])
                            for mb in range(4):
                                view, wk = w_next("wo")
                                wo_ = view(0, 8, 256)
                                for sub in range(2):
                                    m = mb * 2 + sub
                                    for (p0, pn) in pieces:
                                        bank, bkey = nbf()
                                        mm_group(bank[:, 0:pn], [(wo_[:, kc, sub * 128:(sub + 1) * 128],
                                                                  mrg[:, kc, p0:p0 + pn]) for kc in range(KC)],
                                                 wk + [("mrg", kc) for kc in range(KC)], bkey)
                                        xk = tkeys("xT", p0, pn)
                                        T.op("dve", lambda bank=bank, m=m, p0=p0, pn=pn: V.tensor_tensor(
                                            out=xT[:, m, p0:p0 + pn], in0=bank[:, 0:pn], in1=xT[:, m, p0:p0 + pn],
                                            op=ALU.add), [bkey] + xk, xk)
                            T.barrier()
                        T.barrier()
                    T.barrier()

            if cfg["ffn1"]:
                with contextlib.ExitStack() as scope:
                    ffn(0, scope)
                    T.barrier()

            if cfg["mix"]:
                mix()

            if cfg["ffn2"]:
                with contextlib.ExitStack() as scope:
                    ffn(2, scope)
                    T.barrier()

            with contextlib.ExitStack() as scope:
                yo = sb("yo", [128, 2, D], F32, scope)
                junk2 = sb("junk2", [128, 512], F32, scope)
                ssq = sb("ssq", [128, 2, 4], F32, scope)
                gfin_b = sb("gfin_b", [128, D], F32, scope)
                T.dma("sp", gfin_b[:], norm_final.partition_broadcast(128), writes=["gfin_b"], group=("gfin", 0))
                fb = {}

                def fA(ti):
                    s2 = ti % 2
                    banks = []
                    for half in range(2):
                        bank, bkey = nbf(hold=True)
                        tr_group([(bank[:, c4 * 128:(c4 + 1) * 128], xT[:, half * 4 + c4, ti * 128:(ti + 1) * 128])
                                  for c4 in range(4)], ident_f[:], [("xT", ti), "ident_f"], bkey)
                        T.op("act", lambda bank=bank, half=half, s2=s2: A_.activation(
                            out=junk2[:], in_=bank[:], func=AF.Square, accum_out=ssq[:, s2, half:half + 1]),
                            [bkey], ["junk2", ("ssq", s2)])
                        banks.append((bank, bkey))
                    fb[ti] = banks
                    T.op("dve", lambda s2=s2: V.tensor_tensor(out=ssq[:, s2, 2:3], in0=ssq[:, s2, 0:1],
                                                              in1=ssq[:, s2, 1:2], op=ALU.add),
                         [("ssq", s2)], [("ssq", s2)])

                def fB(ti):
                    tile = tiles[ti]
                    s2 = ti % 2
                    banks = fb.pop(ti)
                    T.op("act", lambda s2=s2: A_.activation(out=ssq[:, s2, 3:4], in_=ssq[:, s2, 2:3], func=AF.Sqrt,
                                                            scale=1.0 / D, bias=eps_col[:, 0:1]),
                         [("ssq", s2), "eps_col"], [("ssq", s2)])
                    T.op("dve", lambda s2=s2: V.reciprocal(out=ssq[:, s2, 3:4], in_=ssq[:, s2, 3:4]),
                         [("ssq", s2)], [("ssq", s2)])
                    for half in range(2):
                        bank, bkey = banks[half]
                        T.op("dve", lambda bank=bank, half=half, s2=s2: V.scalar_tensor_tensor(
                            out=yo[:, s2, half * 512:(half + 1) * 512], in0=bank[:], scalar=ssq[:, s2, 3:4],
                            in1=gfin_b[:, half * 512:(half + 1) * 512], op0=ALU.mult, op1=ALU.mult),
                            [bkey, ("ssq", s2), "gfin_b"], [("yo", s2)])
                        release(bkey)
                    dst = y_p[tile * 128:(tile + 1) * 128, :] if tile < NPT else y_s[:, :]
                    T.dma("sp", dst, yo[:, s2, :], reads=[("yo", s2)], group=("yo", s2))

                fA(0)
                for ti in range(nt):
                    if ti + 1 < nt:
                        fA(ti + 1)
                    fB(ti)
                T.barrier()

        for pi, tiles in enumerate(PASSES):
            emit_pass(pi, tiles)
        T.final_wait()
        build_nc.stats = (T.n_ops, T.n_wait, T.nsem)
    return nc


_IN_NAMES = ["norm_ffn1", "ffn1_w_in", "ffn1_w_out", "norm_mix", "w_in", "pool_w_group", "pool_scale",
             "pool_w_out", "conv_w", "conv_b", "dt_bias", "a_log", "d_skip", "ssd_norm", "ssd_w_out", "w_o",
             "norm_ffn2", "ffn2_w_in", "ffn2_w_out"]


def kernel(**inputs):
    n = 8
    f = lambda a: np.ascontiguousarray(np.asarray(a, dtype=np.float32))
    shared = {k: f(inputs[k])[0] for k in _IN_NAMES}
    shared["norm_final"] = f(inputs["norm_final"])
    x_prompt = f(inputs["x_prompt"])
    x_sample = f(inputs["x_sample"])
    state_pool = f(inputs["state_pool"])[0]
    state_conv = f(inputs["state_conv"])[0]
    state_ssm = f(inputs["state_ssm"])[0]
    in_maps = []
    for c in range(n):
        m = dict(shared)
        m["xp"] = x_prompt[c]
        m["xs"] = np.ascontiguousarray(x_sample[c * NSEQ:(c + 1) * NSEQ].reshape(NSEQ * TS, D))
        m["st_pool"] = np.ascontiguousarray(state_pool[c * NSEQ:(c + 1) * NSEQ])
        m["st_conv"] = np.ascontiguousarray(state_conv[c * NSEQ:(c + 1) * NSEQ])
        m["st_ssm"] = np.ascontiguousarray(state_ssm[c * NSEQ:(c + 1) * NSEQ].reshape(NSEQ, DIN, NST))
        in_maps.append(m)
    nc = build_nc()
    res = run_bass_kernel_spmd(nc, in_maps, core_ids=list(range(n)))
    R = res.results
    y_prompt = np.stack([R[c]["y_p"] for c in range(n)], 0)
    y_sample = np.concatenate([R[c]["y_s"].reshape(NSEQ, TS, D) for c in range(n)], 0)
    pool_p = np.stack([R[c]["o_pool_p"] for c in range(n)], 0)[None]
    conv_p = np.stack([R[c]["o_conv_p"] for c in range(n)], 0)[None]
    ssm_p = np.stack([R[c]["o_ssm_p"].reshape(NH, HP, NST) for c in range(n)], 0)[None]
    pool_s = np.concatenate([R[c]["o_pool_s"] for c in range(n)], 0)[None]
    conv_s = np.concatenate([R[c]["o_conv_s"] for c in range(n)], 0)[None]
    ssm_s = np.concatenate([R[c]["o_ssm_s"].reshape(NSEQ, NH, HP, NST) for c in range(n)], 0)[None]
    return (y_prompt, y_sample, pool_p, conv_p, ssm_p, pool_s, conv_s, ssm_s)
```

```python
import contextlib
import os
import numpy as np
import concourse.bass as bass
import concourse.mybir as mybir
from concourse.bass_utils import run_bass_kernel_spmd

F32 = mybir.dt.float32
BF16 = mybir.dt.bfloat16
ALU = mybir.AluOpType
AF = mybir.ActivationFunctionType

ENGS = ("pe", "act", "dve", "pool", "sp")
SEM_ROT = 30000


class Trk:
    def __init__(self, nc, stack):
        self.nc = nc
        self.stack = stack
        self.eng = {"pe": nc.tensor, "act": nc.scalar, "dve": nc.vector,
                    "pool": nc.gpsimd, "sp": nc.sync}
        self.sem = {}
        self.cnt = {}
        self.nsem = 0
        for e in ENGS:
            self._new_eng_sem(e)
        self.waited = {e: {} for e in ENGS}
        self.last_w = {}
        self.readers = {}
        self.dma_groups = {}
        self.n_wait = 0
        self.n_ops = 0

    def _alloc_sem(self, name):
        s = self.stack.enter_context(self.nc.semaphore(name))
        self.nsem += 1
        return s

    def _new_eng_sem(self, e):
        self.sem[e] = self._alloc_sem(f"s_{e}_{self.nsem}")
        self.cnt[e] = 0

    def _deps(self, eng, reads, writes):
        deps = []
        for k in reads:
            t = self.last_w.get(k)
            if t is not None:
                if t[2] != eng or eng in ("act", "dve", "pool", "dma"):
                    deps.append(t)
        for k in writes:
            t = self.last_w.get(k)
            if t is not None and (t[2] != eng or eng == "dma"):
                deps.append(t)
            for t in self.readers.get(k, {}).values():
                if t[2] != eng or eng == "dma":
                    deps.append(t)
        return deps

    def _emit_waits(self, eng, deps):
        w = self.waited[eng]
        best = {}
        for (s, v, _) in deps:
            sid = id(s)
            if w.get(sid, 0) >= v:
                continue
            if sid not in best or best[sid][1] < v:
                best[sid] = (s, v)
        for sid, (s, v) in best.items():
            self.eng[eng].wait_ge(s, v)
            w[sid] = v
            self.n_wait += 1

    def _record(self, tok, reads, writes):
        for k in writes:
            self.last_w[k] = tok
            self.readers[k] = {}
        for k in reads:
            r = self.readers.setdefault(k, {})
            key = id(tok[0])
            if key not in r or r[key][1] < tok[1]:
                r[key] = tok

    def op(self, eng, fn, reads=(), writes=(), inc=True):
        self.n_ops += 1
        deps = self._deps(eng, reads, writes)
        self._emit_waits(eng, deps)
        ins = fn()
        if inc:
            if self.cnt[eng] >= SEM_ROT:
                self._new_eng_sem(eng)
            self.cnt[eng] += 1
            ins.then_inc(self.sem[eng], 1)
            tok = (self.sem[eng], self.cnt[eng], eng)
        else:
            assert self.cnt[eng] < SEM_ROT
            tok = (self.sem[eng], self.cnt[eng] + 1, eng)
        self._record(tok, reads, writes)
        return ins

    def dma(self, q, out, in_, reads=(), writes=(), group=None, **kw):
        self.n_ops += 1
        deps = self._deps("dma", reads, writes)
        self._emit_waits(q, deps)
        g = self.dma_groups.get(group)
        if g is None:
            g = [self._alloc_sem(f"d_{self.nsem}"), 0]
            self.dma_groups[group] = g
        ins = self.eng[q].dma_start(out=out, in_=in_, **kw)
        g[1] += 16
        ins.then_inc(g[0], 16)
        tok = (g[0], g[1], "dma")
        self._record(tok, reads, writes)
        return ins

    def all_tokens(self, skip_groups=()):
        toks = []
        for e in ENGS:
            if self.cnt[e] > 0:
                toks.append((self.sem[e], self.cnt[e], e))
        for k, g in self.dma_groups.items():
            if g[1] > 0 and not (isinstance(k, tuple) and k and k[0] in skip_groups):
                toks.append((g[0], g[1], "dma"))
        return toks

    def barrier(self, skip_groups=("w",)):
        toks = self.all_tokens(skip_groups)
        for e in ENGS:
            self._emit_waits(e, toks)

    def final_wait(self):
        self._emit_waits("sp", self.all_tokens())


D = 1024
KC = 8
DFF = 2816
NJ = 22
DIN = 2048
CONVD = 3072
NH = 32
HP = 64
NST = 128
NG = 4
IN_PROJ = 8224
OFF_U, OFF_Z, OFF_X, OFF_B, OFF_C, OFF_DT, OFF_G0, OFF_G1 = 0, 1024, 3072, 5120, 5632, 6144, 6176, 7200
EPS = 1e-6
NPT = 16
STILE = 16
NSEQ = 16
TS = 8
PASSES = [list(range(0, 6)), list(range(6, 12)), list(range(12, 17))]
NTMAX = 768
NSLOT = 3
SLOTW = 5632
NEG = -30000.0

CFG = {
    "mix": int(os.environ.get("K_MIX", "1")),
    "ffn1": int(os.environ.get("K_FFN1", "1")),
    "ffn2": int(os.environ.get("K_FFN2", "1")),
}


def build_nc(cfg=CFG):
    nc = bass.Bass("TRN2", target_bir_lowering=False)

    def din(name, shape):
        return nc.dram_tensor(name, list(shape), F32, kind="ExternalInput").ap()

    def dout(name, shape):
        return nc.dram_tensor(name, list(shape), F32, kind="ExternalOutput").ap()

    xp = din("xp", [2048, D])
    xs = din("xs", [128, D])
    st_pool = din("st_pool", [NSEQ, 15, D])
    st_conv = din("st_conv", [NSEQ, 3, CONVD])
    st_ssm = din("st_ssm", [NSEQ, DIN, NST])
    norm_ffn1 = din("norm_ffn1", [D])
    ffn1_w_in = din("ffn1_w_in", [D, 2 * DFF])
    ffn1_w_out = din("ffn1_w_out", [DFF, D])
    norm_mix = din("norm_mix", [D])
    w_in = din("w_in", [D, IN_PROJ])
    pool_w_group = din("pool_w_group", [NG, 256, 256])
    pool_scale = din("pool_scale", [D])
    pool_w_out = din("pool_w_out", [D, D])
    conv_w = din("conv_w", [4, CONVD])
    conv_b = din("conv_b", [CONVD])
    dt_bias = din("dt_bias", [NH])
    a_log = din("a_log", [NH])
    d_skip = din("d_skip", [NH])
    ssd_norm = din("ssd_norm", [DIN])
    ssd_w_out = din("ssd_w_out", [DIN, D])
    w_o = din("w_o", [D, D])
    norm_ffn2 = din("norm_ffn2", [D])
    ffn2_w_in = din("ffn2_w_in", [D, 2 * DFF])
    ffn2_w_out = din("ffn2_w_out", [DFF, D])
    norm_final = din("norm_final", [D])

    y_p = dout("y_p", [2048, D])
    y_s = dout("y_s", [128, D])
    o_pool_p = dout("o_pool_p", [15, D])
    o_conv_p = dout("o_conv_p", [3, CONVD])
    o_ssm_p = dout("o_ssm_p", [DIN, NST])
    o_pool_s = dout("o_pool_s", [NSEQ, 15, D])
    o_conv_s = dout("o_conv_s", [NSEQ, 3, CONVD])
    o_ssm_s = dout("o_ssm_s", [NSEQ, DIN, NST])

    with contextlib.ExitStack() as st:
        T = Trk(nc, st)
        V, A_, P_ = nc.vector, nc.scalar, nc.gpsimd

        uniq = [0]

        def sb(name, shape, dt=F32, stack=st):
            uniq[0] += 1
            return stack.enter_context(nc.sbuf_tensor(f"{name}_{uniq[0]}", list(shape), dt))

        psf = [st.enter_context(nc.psum_tensor(f"psf{i}", [128, 512], F32)) for i in range(6)]
        psb = [st.enter_context(nc.psum_tensor(f"psb{i}", [128, 1024], BF16)) for i in range(2)]
        rr = {"f": 0, "b": 0}
        held = set()

        def nbf(hold=False):
            while True:
                i = rr["f"] % 6
                rr["f"] += 1
                if i not in held:
                    break
            if hold:
                held.add(i)
            return psf[i], ("psf", i)

        def release(key):
            held.discard(key[1])

        def nbb():
            i = rr["b"] % 2
            rr["b"] += 1
            return psb[i], ("psb", i)

        ones_f = sb("ones_f", [128, 128])
        zeros_f = sb("zeros_f", [128, 128])
        ident_f = sb("ident_f", [128, 128])
        ident_b = sb("ident_b", [128, 128], BF16)
        ones_b = sb("ones_b", [128, 128], BF16)
        tri_f = sb("tri_f", [128, 2, 128])
        tri_b = sb("tri_b", [128, 2, 128], BF16)
        negm = sb("negm", [128, 2, 4, 128], BF16)
        negm_f = sb("negm_f", [128, 2, 128])
        smask = sb("smask", [128, NSEQ, 128], BF16)
        rowmask = sb("rowmask", [128, NSEQ])
        gcols = sb("gcols", [128, 3, 8])
        pscol = sb("pscol", [128, 8])
        cwcol = sb("cwcol", [128, 4, 24])
        cbcol = sb("cbcol", [128, 24])
        blk_f = sb("blk_f", [128, 128])
        dtb_b = sb("dtb_b", [128, NH])
        A_b = sb("A_b", [128, NH])
        D_b = sb("D_b", [128, NH])
        invc = sb("invc", [128, 16])
        eps_col = sb("eps_col", [128, 2])
        Ddiag = sb("Ddiag", [128, NH, 128], BF16)

        def pool_op(fn, reads, writes):
            return T.op("pool", fn, reads, writes)

        pool_op(lambda: P_.memset(ones_f[:], 1.0), [], ["ones_f"])
        pool_op(lambda: P_.memset(zeros_f[:], 0.0), [], ["zeros_f"])
        pool_op(lambda: P_.memset(ones_b[:], 1.0), [], ["ones_b"])
        pool_op(lambda: P_.memset(eps_col[:, 0:1], EPS), [], ["eps_col"])
        pool_op(lambda: P_.memset(eps_col[:, 1:2], 4.0 * EPS), [], ["eps_col"])
        pool_op(lambda: P_.affine_select(out=ident_f[:], in_=ones_f[:], pattern=[[-1, 128]],
                                         compare_op=ALU.is_equal, fill=0.0, base=0, channel_multiplier=1),
                ["ones_f"], ["ident_f"])
        pool_op(lambda: P_.tensor_copy(out=ident_b[:], in_=ident_f[:]), ["ident_f"], ["ident_b"])
        pool_op(lambda: P_.affine_select(out=tri_f[:, 0, :], in_=ones_f[:], pattern=[[1, 128]],
                                         compare_op=ALU.is_ge, fill=0.0, base=0, channel_multiplier=-1),
                ["ones_f"], ["tri_f"])
        pool_op(lambda: P_.affine_select(out=negm_f[:, 0, :], in_=zeros_f[:], pattern=[[1, 128]],
                                         compare_op=ALU.is_ge, fill=NEG, base=0, channel_multiplier=-1),
                ["zeros_f"], ["negm_f"])
        t3 = tri_f[:, 1, :].rearrange("p (a b) -> p a b", b=8)
        n3 = negm_f[:, 1, :].rearrange("p (a b) -> p a b", b=8)
        o3 = ones_f[:].rearrange("p (a b) -> p a b", b=8)
        z3 = zeros_f[:].rearrange("p (a b) -> p a b", b=8)
        pool_op(lambda: P_.affine_select(out=t3, in_=o3, pattern=[[8, 16], [1, 8]],
                                         compare_op=ALU.is_ge, fill=0.0, base=0, channel_multiplier=-1),
                ["ones_f"], ["tri_f"])
        pool_op(lambda: P_.affine_select(out=t3, in_=t3, pattern=[[-8, 16], [0, 8]],
                                         compare_op=ALU.is_ge, fill=0.0, base=0, channel_multiplier=1),
                ["tri_f"], ["tri_f"])
        pool_op(lambda: P_.affine_select(out=n3, in_=z3, pattern=[[8, 16], [1, 8]],
                                         compare_op=ALU.is_ge, fill=NEG, base=0, channel_multiplier=-1),
                ["zeros_f"], ["negm_f"])
        pool_op(lambda: P_.affine_select(out=n3, in_=n3, pattern=[[-8, 16], [0, 8]],
                                         compare_op=ALU.is_ge, fill=NEG, base=0, channel_multiplier=1),
                ["negm_f"], ["negm_f"])
        pool_op(lambda: P_.tensor_copy(out=tri_b[:], in_=tri_f[:]), ["tri_f"], ["tri_b"])
        for k in range(2):
            pool_op(lambda k=k: P_.tensor_copy(out=negm[:, k],
                                               in_=negm_f[:, k, :].unsqueeze(1).to_broadcast([128, 4, 128])),
                    ["negm_f"], ["negm"])
        pool_op(lambda: P_.memset(smask[:], 1.0), [], ["smask"])
        pool_op(lambda: P_.affine_select(out=smask[:], in_=smask[:], pattern=[[-8, NSEQ], [1, 128]],
                                         compare_op=ALU.is_ge, fill=0.0, base=0, channel_multiplier=0),
                ["smask"], ["smask"])
        pool_op(lambda: P_.affine_select(out=smask[:], in_=smask[:], pattern=[[8, NSEQ], [-1, 128]],
                                         compare_op=ALU.is_ge, fill=0.0, base=7, channel_multiplier=0),
                ["smask"], ["smask"])
        b3 = blk_f[:].rearrange("p (a b) -> p a b", b=8)
        pool_op(lambda: P_.affine_select(out=b3, in_=o3, pattern=[[-8, 16], [0, 8]],
                                         compare_op=ALU.is_ge, fill=0.0, base=0, channel_multiplier=1),
                ["ones_f"], ["blk_f"])
        pool_op(lambda: P_.affine_select(out=b3, in_=b3, pattern=[[8, 16], [0, 8]],
                                         compare_op=ALU.is_ge, fill=0.0, base=7, channel_multiplier=-1),
                ["blk_f"], ["blk_f"])
        pool_op(lambda: P_.memset(rowmask[:], 1.0), [], ["rowmask"])
        pool_op(lambda: P_.affine_select(out=rowmask[:], in_=rowmask[:], pattern=[[-8, NSEQ]],
                                         compare_op=ALU.is_ge, fill=0.0, base=0, channel_multiplier=1),
                ["rowmask"], ["rowmask"])
        pool_op(lambda: P_.affine_select(out=rowmask[:], in_=rowmask[:], pattern=[[8, NSEQ]],
                                         compare_op=ALU.is_ge, fill=0.0, base=7, channel_multiplier=-1),
                ["rowmask"], ["rowmask"])
        pool_op(lambda: P_.iota(invc[:], [[1, 16]], base=1, channel_multiplier=0,
                                allow_small_or_imprecise_dtypes=True), [], ["invc"])
        T.op("dve", lambda: V.reciprocal(out=invc[:], in_=invc[:]), ["invc"], ["invc"])

        with nc.allow_non_contiguous_dma(reason="small param columns"):
            for i, g in enumerate((norm_ffn1, norm_mix, norm_ffn2)):
                T.dma("sp", gcols[:, i, :], g.rearrange("(c p) -> p c", p=128), writes=["gcols"], group=("c", 0))
            T.dma("sp", pscol[:], pool_scale.rearrange("(c p) -> p c", p=128), writes=["pscol"], group=("c", 1))
            T.dma("sp", cbcol[:], conv_b.rearrange("(c p) -> p c", p=128), writes=["cbcol"], group=("c", 2))
            for k in range(4):
                T.dma("sp", cwcol[:, k, :], conv_w[k].rearrange("(c p) -> p c", p=128), writes=["cwcol"],
                      group=("c", 3))
        T.dma("sp", dtb_b[:], dt_bias.partition_broadcast(128), writes=["dtb_b"], group=("c", 6))
        T.dma("sp", A_b[:], a_log.partition_broadcast(128), writes=["A_b"], group=("c", 7))
        T.dma("sp", D_b[:], d_skip.partition_broadcast(128), writes=["D_b"], group=("c", 8))
        T.op("dve", lambda: V.tensor_tensor(out=Ddiag[:], in0=ident_f[:].unsqueeze(1).to_broadcast([128, NH, 128]),
                                            in1=D_b[:].unsqueeze(2).to_broadcast([128, NH, 128]), op=ALU.mult),
             ["ident_f", "D_b"], ["Ddiag"])
        T.op("act", lambda: A_.activation(out=A_b[:], in_=A_b[:], func=AF.Exp), ["A_b"], ["A_b"])
        T.op("dve", lambda: V.tensor_scalar(out=A_b[:], in0=A_b[:], scalar1=-1.0, scalar2=None, op0=ALU.mult),
             ["A_b"], ["A_b"])

        xT = sb("xT", [128, KC, NTMAX])
        xn = sb("xn", [128, KC, NTMAX], BF16)
        wsl = sb("wsl", [128, NSLOT, SLOTW], BF16)
        rs = sb("rs", [128, 512])
        hT = sb("hT", [128, NG, 512])
        hTb = sb("hTb", [128, NG, 512], BF16)
        convhist = sb("convhist", [128, 24, 3])
        poolhist = sb("poolhist", [128, 8, 15])
        T.op("pool", lambda: P_.memset(hT[:], 0.0), [], [("hT", g) for g in range(NG)])
        T.op("pool", lambda: P_.memset(hTb[:], 0.0), [], [("hTb", g) for g in range(NG)])
        T.op("pool", lambda: P_.memset(convhist[:], 0.0), [], [("convhist", c) for c in range(24)])
        T.op("pool", lambda: P_.memset(poolhist[:], 0.0), [], [("poolhist", c) for c in range(8)])

        def wview(W, c0, n):
            return W[:, c0:c0 + n].rearrange("(kc p) n -> p kc n", p=128)

        def pass_specs():
            sp_ = []

            def ffn(w_i, w_o_):
                for jb in range(NJ // 2):
                    sp_.append(("ffn_in", [(wview(w_i, jb * 256, 256), 0, 8, 256),
                                           (wview(w_i, DFF + jb * 256, 256), 2048, 8, 256)]))
                for mb in range(4):
                    sp_.append(("ffn_out", [(wview(w_o_, mb * 256, 256), 0, NJ, 256)]))
            if cfg["ffn1"]:
                ffn(ffn1_w_in, ffn1_w_out)
            if cfg["mix"]:
                sp_.append(("dt", [(wview(w_in, OFF_DT, 32), 0, 8, 32)]))
                for g in range(NG):
                    sp_.append(("xa", [(wview(w_in, OFF_X + g * 512, 256), 0, 8, 256)]))
                    sp_.append(("xb", [(wview(w_in, OFF_X + g * 512 + 256, 256), 0, 8, 256)]))
                    sp_.append(("bc", [(wview(w_in, OFF_B + g * 128, 128), 0, 8, 128),
                                       (wview(w_in, OFF_C + g * 128, 128), 1024, 8, 128)]))
                    sp_.append(("z", [(wview(w_in, OFF_Z + g * 512, 512), 0, 8, 512)]))
                for ub in range(4):
                    sp_.append(("u", [(wview(w_in, OFF_U + ub * 256, 256), 0, 8, 256)]))
                for mb in range(4):
                    sp_.append(("g1", [(wview(w_in, OFF_G1 + mb * 256, 256), 0, 8, 256)]))
                    sp_.append(("sso", [(wview(ssd_w_out, mb * 256, 256), 0, 16, 256)]))
                    sp_.append(("g0", [(wview(w_in, OFF_G0 + mb * 256, 256), 0, 8, 256)]))
                    sp_.append(("pwo", [(wview(pool_w_out, mb * 256, 256), 0, 8, 256)]))
                for mb in range(4):
                    sp_.append(("wo", [(wview(w_o, mb * 256, 256), 0, 8, 256)]))
            if cfg["ffn2"]:
                ffn(ffn2_w_in, ffn2_w_out)
            return sp_

        specs = []
        for _ in PASSES:
            specs += pass_specs()
        wst = {"issued": 0, "consumed": 0}

        def w_issue(i):
            slot = i % NSLOT
            for pi, (src, off, k, n) in enumerate(specs[i][1]):
                dst = wsl[:, slot, off:off + k * n].rearrange("p (k n) -> p k n", k=k)
                T.dma("pool", dst, src, writes=[("w", slot, pi)], group=("w", slot, pi))

        def w_next(tag):
            i = wst["consumed"]
            assert specs[i][0] == tag, (specs[i][0], tag)
            while wst["issued"] < min(len(specs), i + NSLOT):
                w_issue(wst["issued"])
                wst["issued"] += 1
            wst["consumed"] += 1
            slot = i % NSLOT
            keys = [("w", slot, pi) for pi in range(len(specs[i][1]))]

            def view(off, k, n):
                return wsl[:, slot, off:off + k * n].rearrange("p (k n) -> p k n", k=k)
            return view, keys

        def mm_group(out, pairs, reads, wkey):
            n = len(pairs)
            for i, (l, r) in enumerate(pairs):
                T.op("pe", lambda l=l, r=r, i=i: nc.tensor.matmul(out, lhsT=l, rhs=r, start=(i == 0),
                                                                  stop=(i == n - 1)),
                     reads, [wkey], inc=(i == n - 1))

        def tr_group(outs_ins, ident, reads, wkey):
            n = len(outs_ins)
            for i, (o, a) in enumerate(outs_ins):
                T.op("pe", lambda o=o, a=a: nc.tensor.transpose(out=o, in_=a, identity=ident),
                     reads, [wkey], inc=(i == n - 1))

        def bc_mid(ap2, n):
            return ap2.unsqueeze(2).to_broadcast([128, ap2.shape[1], n])

        def bc_out(ap2, n):
            return ap2.unsqueeze(1).to_broadcast([128, n, ap2.shape[1]])

        def emit_pass(pi, tiles):
            nt = len(tiles)
            NT = nt * 128
            ptiles = [t for t in tiles if t < NPT]
            npt = len(ptiles)
            NTp = npt * 128
            has_s = STILE in tiles
            first_pass = (tiles[0] == 0)
            more_prompt = (ptiles[-1] < NPT - 1)
            pieces = []
            p0 = 0
            while p0 < NT:
                pn = min(512, NT - p0)
                pieces.append((p0, pn))
                p0 += pn

            def tkeys(name, p0, pn):
                return [(name, ti) for ti in range(p0 // 128, (p0 + pn) // 128)]

            with contextlib.ExitStack() as scope:
                xin = sb("xin", [128, 3, D], F32, scope)
                for ti, tile in enumerate(tiles):
                    src = xp[tile * 128:(tile + 1) * 128, :] if tile < NPT else xs[:, :]
                    slot = ti % 3
                    T.dma("sp", xin[:, slot, :], src, writes=[("xin", slot)], group=("xin", slot))
                    for half in range(2):
                        bank, bkey = nbf()
                        tr_group([(bank[:, c4 * 128:(c4 + 1) * 128],
                                   xin[:, slot, (half * 4 + c4) * 128:(half * 4 + c4 + 1) * 128])
                                  for c4 in range(4)], ident_f[:], [("xin", slot), "ident_f"], bkey)
                        T.op("act", lambda bank=bank, half=half, ti=ti: A_.copy(
                            out=xT[:, half * 4:(half + 1) * 4, ti * 128:(ti + 1) * 128],
                            in_=bank[:].rearrange("p (c t) -> p c t", c=4)), [bkey], [("xT", ti)])
                T.barrier()

            def rmsnorm(gi, scope):
                sq = sb("sq", [128, KC, 512], BF16, scope)
                for (p0, pn) in pieces:
                    xk = tkeys("xT", p0, pn)
                    T.op("pool", lambda p0=p0, pn=pn: P_.tensor_tensor(
                        out=sq[:, 0:4, 0:pn], in0=xT[:, 0:4, p0:p0 + pn], in1=xT[:, 0:4, p0:p0 + pn], op=ALU.mult),
                        xk, [("sq", 0)])
                    T.op("act", lambda p0=p0, pn=pn: A_.activation(
                        out=sq[:, 4:8, 0:pn], in_=xT[:, 4:8, p0:p0 + pn], func=AF.Square), xk, [("sq", 1)])
                    bank, bkey = nbf()
                    mm_group(bank[:, 0:pn], [(ones_b[:], sq[:, c, 0:pn]) for c in range(KC)],
                             [("sq", 0), ("sq", 1), "ones_b"], bkey)
                    T.op("act", lambda bank=bank, pn=pn: A_.activation(
                        out=rs[:, 0:pn], in_=bank[:, 0:pn], func=AF.Sqrt, scale=1.0 / D, bias=eps_col[:, 0:1]),
                        [bkey, "eps_col"], ["rs"])
                    T.op("dve", lambda pn=pn: V.reciprocal(out=rs[:, 0:pn], in_=rs[:, 0:pn]), ["rs"], ["rs"])
                    for c in range(KC):
                        T.op("dve", lambda c=c, p0=p0, pn=pn: V.scalar_tensor_tensor(
                            out=xn[:, c, p0:p0 + pn], in0=xT[:, c, p0:p0 + pn], scalar=gcols[:, gi, c:c + 1],
                            in1=rs[:, 0:pn], op0=ALU.mult, op1=ALU.mult),
                            xk + ["rs", "gcols"], tkeys("xn", p0, pn))

            def ffn(gi, scope):
                rmsnorm(gi, scope)
                act = sb("act", [128, NJ, NTMAX], BF16, scope)
                sg = sb("sg", [128, 2, 512], F32, scope)
                cnt = 0
                for jb in range(NJ // 2):
                    view, wk = w_next("ffn_in")
                    wg = view(0, 8, 256)
                    wu = view(2048, 8, 256)
                    for sub in range(2):
                        j = jb * 2 + sub
                        for pidx, (p0, pn) in enumerate(pieces):
                            xk = tkeys("xn", p0, pn)
                            bg, bgk = nbf()
                            bu, buk = nbf()
                            mm_group(bg[:, 0:pn], [(wg[:, kc, sub * 128:(sub + 1) * 128], xn[:, kc, p0:p0 + pn])
                                                   for kc in range(KC)], wk + xk, bgk)
                            mm_group(bu[:, 0:pn], [(wu[:, kc, sub * 128:(sub + 1) * 128], xn[:, kc, p0:p0 + pn])
                                                   for kc in range(KC)], wk + xk, buk)
                            s2 = cnt % 2
                            cnt += 1
                            T.op("act", lambda bg=bg, pn=pn, s2=s2: A_.activation(
                                out=sg[:, s2, 0:pn], in_=bg[:, 0:pn], func=AF.Silu), [bgk], [("sg", s2)])
                            T.op("dve", lambda bu=bu, pn=pn, p0=p0, j=j, s2=s2: V.tensor_tensor(
                                out=act[:, j, p0:p0 + pn], in0=bu[:, 0:pn], in1=sg[:, s2, 0:pn], op=ALU.mult),
                                [buk, ("sg", s2)], [("act", j, pidx)])
                for mb in range(4):
                    view, wk = w_next("ffn_out")
                    wo = view(0, NJ, 256)
                    for sub in range(2):
                        m = mb * 2 + sub
                        for pidx, (p0, pn) in enumerate(pieces):
                            bank, bkey = nbf()
                            mm_group(bank[:, 0:pn], [(wo[:, j, sub * 128:(sub + 1) * 128], act[:, j, p0:p0 + pn])
                                                     for j in range(NJ)],
                                     wk + [("act", j, pidx) for j in range(NJ)], bkey)
                            xk = tkeys("xT", p0, pn)
                            T.op("dve", lambda bank=bank, m=m, p0=p0, pn=pn: V.scalar_tensor_tensor(
                                out=xT[:, m, p0:p0 + pn], in0=bank[:, 0:pn], scalar=0.5, in1=xT[:, m, p0:p0 + pn],
                                op0=ALU.mult, op1=ALU.add), [bkey] + xk, xk)

            def mix():
                spec = [(ti, t) for ti, t in enumerate(tiles) if t == NPT - 1 or t == STILE]
                with contextlib.ExitStack() as ms:
                    with contextlib.ExitStack() as s0:
                        rmsnorm(1, s0)
                        T.barrier()
                    ynT = sb("ynT", [128, 16, NTMAX], BF16, ms)
                    stage = sb("stage", [128, 2, 256], F32, ms)
                    stcnt = [0]

                    def tokmajor_out(wblk, wk, ncols, kind, col0):
                        for (ti, t) in spec:
                            bank, bkey = nbf()
                            mm_group(bank[:, 0:ncols], [(xn[:, kc, ti * 128:(ti + 1) * 128], wblk[:, kc, :])
                                                        for kc in range(KC)], wk + [("xn", ti)], bkey)
                            s2 = stcnt[0] % 2
                            stcnt[0] += 1
                            T.op("act", lambda bank=bank, s2=s2: A_.copy(out=stage[:, s2, 0:ncols],
                                                                         in_=bank[:, 0:ncols]),
                                 [bkey], [("stage", s2)])
                            if kind == "conv":
                                if t == NPT - 1:
                                    T.dma("sp", o_conv_p[0:3, col0:col0 + ncols], stage[125:128, s2, 0:ncols],
                                          reads=[("stage", s2)], group=("stg", s2))
                                else:
                                    for r in range(3):
                                        T.dma("sp", o_conv_s[:, r, col0:col0 + ncols],
                                              stage[5 + r::8, s2, 0:ncols], reads=[("stage", s2)],
                                              group=("stg", s2))
                            else:
                                if t == NPT - 1:
                                    T.dma("sp", o_pool_p[0:15, col0:col0 + ncols], stage[113:128, s2, 0:ncols],
                                          reads=[("stage", s2)], group=("stg", s2))
                                else:
                                    T.dma("sp", o_pool_s[:, 7:15, col0:col0 + ncols], stage[:, s2, 0:ncols],
                                          reads=[("stage", s2)], group=("stg", s2))

                    with contextlib.ExitStack() as sa:
                        def sba(name, shape, dt=F32):
                            return sb(name, shape, dt, sa)
                        dtr = sba("dtr", [128, nt, NH])
                        dt_ = sba("dt_", [128, nt, NH])
                        dtA = sba("dtA", [128, nt, NH])
                        acs = sba("acs", [128, nt, NH])
                        nacs = sba("nacs", [128, nt, NH])
                        eacs = sba("eacs", [128, nt, NH])
                        dend = sba("dend", [128, nt, NH])
                        decb = sba("decb", [128, nt, NH])
                        hi = sba("hi", [128, nt, NH], BF16)
                        lo = sba("lo", [128, nt, NH], BF16)
                        x_tok = sba("x_tok", [128, nt, 512], BF16)
                        zt_all = sba("zt_all", [128, nt, 512], BF16)
                        BTg = sba("BTg", [128, NTMAX], BF16)
                        CTg = sba("CTg", [128, NTMAX], BF16)
                        Btok = sba("Btok", [128, nt, 128], BF16)
                        if has_s:
                            shist = sba("shist", [128, 24, NSEQ * 3])
                            with contextlib.ExitStack() as sst:
                                stc = sb("stc", [128, 2, 512], F32, sst)
                                for cb in range(6):
                                    s2 = cb % 2
                                    T.dma("sp", stc[0:48, s2, :],
                                          st_conv[:, :, cb * 512:(cb + 1) * 512].rearrange("s r c -> (s r) c"),
                                          writes=[("stc", s2)], group=("stc", s2))
                                    bank, bkey = nbf()
                                    tr_group([(bank[:, j * 48:(j + 1) * 48], stc[0:48, s2, j * 128:(j + 1) * 128])
                                              for j in range(4)], ident_f[0:48, 0:48], [("stc", s2), "ident_f"], bkey)
                                    T.op("act", lambda bank=bank, cb=cb: A_.copy(
                                        out=shist[:, cb * 4:(cb + 1) * 4, :],
                                        in_=bank[:, 0:192].rearrange("p (j q) -> p j q", j=4)), [bkey], ["shist"])
                                T.barrier()

                        view, wk = w_next("dt")
                        wdt = view(0, 8, 32)
                        bank, bkey = nbf()
                        for ti in range(nt):
                            mm_group(bank[:, ti * 32:(ti + 1) * 32],
                                     [(xn[:, kc, ti * 128:(ti + 1) * 128], wdt[:, kc, :]) for kc in range(KC)],
                                     wk + [("xn", ti)], bkey)
                        T.op("dve", lambda bank=bank: V.tensor_tensor(
                            out=dtr[:], in0=bank[:, 0:nt * 32].rearrange("p (t h) -> p t h", h=NH),
                            in1=bc_out(dtb_b[:], nt), op=ALU.add), [bkey, "dtb_b"], ["dtr"])
                        T.op("act", lambda: A_.activation(out=dtr[:], in_=dtr[:], func=AF.Exp), ["dtr"], ["dtr"])
                        T.op("act", lambda: A_.activation(out=dt_[:], in_=dtr[:], func=AF.Ln, bias=1.0, scale=1.0),
                             ["dtr"], ["dt_"])
                        T.op("dve", lambda: V.tensor_tensor(out=dtA[:], in0=dt_[:], in1=bc_out(A_b[:], nt),
                                                            op=ALU.mult), ["dt_", "A_b"], ["dtA"])
                        T.op("act", lambda: A_.copy(out=hi[:], in_=dtA[:]), ["dtA"], ["hi"])
                        T.op("dve", lambda: V.tensor_tensor(out=lo[:], in0=dtA[:], in1=hi[:], op=ALU.subtract),
                             ["dtA", "hi"], ["lo"])
                        for ti, t in enumerate(tiles):
                            kind = 1 if t == STILE else 0
                            bank, bkey = nbf()
                            mm_group(bank[:, 0:32], [(tri_f[:, kind, :], dtA[:, ti, :])], ["tri_f", "dtA"], bkey)
                            mm_group(bank[:, 32:64], [((blk_f[:] if kind else ones_f[:]), dtA[:, ti, :])],
                                     ["blk_f", "ones_f", "dtA"], bkey)
                            T.op("act", lambda bank=bank, ti=ti: A_.copy(out=acs[:, ti, :], in_=bank[:, 0:32]),
                                 [bkey], ["acs"])
                            T.op("act", lambda bank=bank, ti=ti: A_.mul(out=nacs[:, ti, :], in_=bank[:, 0:32],
                                                                        mul=-1.0), [bkey], ["nacs"])
                            T.op("act", lambda bank=bank, ti=ti: A_.activation(out=eacs[:, ti, :], in_=bank[:, 0:32],
                                                                               func=AF.Exp), [bkey], ["eacs"])
                            T.op("act", lambda bank=bank, ti=ti: A_.activation(out=decb[:, ti, :],
                                                                               in_=bank[:, 32:64], func=AF.Exp),
                                 [bkey], ["decb"])
                            T.op("dve", lambda bank=bank, ti=ti: V.tensor_tensor(
                                out=dend[:, ti, :], in0=bank[:, 32:64], in1=acs[:, ti, :], op=ALU.subtract),
                                [bkey, "acs"], ["dend"])
                            T.op("act", lambda ti=ti: A_.activation(out=dend[:, ti, :], in_=dend[:, ti, :],
                                                                    func=AF.Exp), ["dend"], ["dend"])

                        for g in range(NG):
                            hs0 = 8 * g
                            with contextlib.ExitStack() as sc:
                                xpre = sb("xpre", [128, 4, 3 + NTMAX], F32, sc)
                                accb = sb("accb", [128, 6, NTMAX], F32, sc)
                                xc = sb("xc", [128, 4, NTMAX], BF16, sc)
                                if has_s:
                                    xpre_s = sb("xpre_s", [128, 4, NSEQ, 3 + TS], F32, sc)
                                ccnt = [0]

                                def conv_chunk(cc, wsl_, wk, dest, dkeys):
                                    s2 = ccnt[0] % 4
                                    s6 = ccnt[0] % 6
                                    ccnt[0] += 1
                                    kx, ka = ("xpre", s2), ("accb", s6)
                                    for (p0, pn) in pieces:
                                        bank, bkey = nbf()
                                        mm_group(bank[:, 0:pn], [(wsl_[:, kc, :], xn[:, kc, p0:p0 + pn])
                                                                 for kc in range(KC)], wk + tkeys("xn", p0, pn), bkey)
                                        if p0 < NTp:
                                            T.op("act", lambda bank=bank, p0=p0, pn=pn: A_.copy(
                                                out=xpre[:, s2, 3 + p0:3 + p0 + pn], in_=bank[:, 0:pn]), [bkey], [kx])
                                        else:
                                            T.op("act", lambda bank=bank: A_.copy(
                                                out=xpre_s[:, s2, :, 3:3 + TS],
                                                in_=bank[:, 0:128].rearrange("p (s t) -> p s t", t=TS)), [bkey], [kx])
                                        T.op("act", lambda bank=bank, p0=p0, pn=pn: A_.activation(
                                            out=accb[:, s6, p0:p0 + pn], in_=bank[:, 0:pn], func=AF.Identity,
                                            scale=cwcol[:, 3, cc:cc + 1], bias=cbcol[:, cc:cc + 1]),
                                            [bkey, "cwcol", "cbcol"], [ka])
                                    if npt:
                                        T.op("pool", lambda: P_.tensor_copy(out=xpre[:, s2, 0:3], in_=convhist[:, cc, :]),
                                             [("convhist", cc)], [kx])
                                        for k in range(3):
                                            T.op("dve", lambda k=k: V.scalar_tensor_tensor(
                                                out=accb[:, s6, 0:NTp], in0=xpre[:, s2, k:k + NTp],
                                                scalar=cwcol[:, k, cc:cc + 1], in1=accb[:, s6, 0:NTp],
                                                op0=ALU.mult, op1=ALU.add), [kx, ka, "cwcol"], [ka])
                                        if more_prompt:
                                            T.op("pool", lambda: P_.tensor_copy(out=convhist[:, cc, :],
                                                                               in_=xpre[:, s2, NTp:NTp + 3]),
                                                 [kx], [("convhist", cc)])
                                    if has_s:
                                        T.op("pool", lambda: P_.tensor_copy(
                                            out=xpre_s[:, s2, :, 0:3],
                                            in_=shist[:, cc, :].rearrange("p (s r) -> p s r", r=3)), ["shist"], [kx])
                                        av = accb[:, s6, NTp:NT].rearrange("p (s t) -> p s t", t=TS)
                                        for k in range(3):
                                            T.op("dve", lambda k=k: V.scalar_tensor_tensor(
                                                out=av, in0=xpre_s[:, s2, :, k:k + TS],
                                                scalar=cwcol[:, k, cc:cc + 1], in1=av,
                                                op0=ALU.mult, op1=ALU.add), [kx, ka, "cwcol"], [ka])
                                    silus.append(lambda: T.op("act", lambda: A_.activation(
                                        out=dest, in_=accb[:, s6, 0:NT], func=AF.Silu), [ka], dkeys))

                                def to_tokmajor(srcT, skeys, dst3, dkeys):
                                    bb, bbk = nbb()
                                    tr_group([(bb[:, ti * 128:(ti + 1) * 128], srcT[:, ti * 128:(ti + 1) * 128])
                                              for ti in range(nt)], ident_b[:], skeys + ["ident_b"], bbk)
                                    T.op("dve", lambda: V.tensor_copy(
                                        out=dst3, in_=bb[:, 0:NT].rearrange("p (t c) -> p t c", c=128)),
                                        [bbk], dkeys)

                                silus = []
                                for ab, tag in enumerate(("xa", "xb")):
                                    view, wk = w_next(tag)
                                    wb = view(0, 8, 256)
                                    for sub in range(2):
                                        j = ab * 2 + sub
                                        cc = 4 * g + j
                                        conv_chunk(cc, wb[:, :, sub * 128:(sub + 1) * 128], wk, xc[:, j, 0:NT],
                                                   [("xc", j)])
                                    tokmajor_out(wb, wk, 256, "conv", g * 512 + ab * 256)
                                view, wk = w_next("bc")
                                wB = view(0, 8, 128)
                                wC = view(1024, 8, 128)
                                conv_chunk(16 + g, wB, wk, BTg[:, 0:NT], ["BTg"])
                                conv_chunk(20 + g, wC, wk, CTg[:, 0:NT], ["CTg"])
                                tokmajor_out(wB, wk, 128, "conv", 2048 + g * 128)
                                tokmajor_out(wC, wk, 128, "conv", 2560 + g * 128)
                                view, wkz = w_next("z")
                                wz = view(0, 8, 512)
                                ztmp = sb("ztmp", [128, 2, 512], F32, sc)
                                for ti in range(nt):
                                    q2 = ti % 2
                                    bz, bzk = nbf()
                                    mm_group(bz[:], [(xn[:, kc, ti * 128:(ti + 1) * 128], wz[:, kc, :]) for kc in range(KC)],
                                             wkz + [("xn", ti)], bzk)
                                    T.op("act", lambda bz=bz, q2=q2: A_.activation(
                                        out=ztmp[:, q2, :], in_=bz[:], func=AF.Tanh, scale=0.5), [bzk], [("ztmp", q2)])
                                    T.op("dve", lambda bz=bz, q2=q2, ti=ti: V.scalar_tensor_tensor(
                                        out=zt_all[:, ti, :], in0=ztmp[:, q2, :], scalar=1.0, in1=bz[:], op0=ALU.add,
                                        op1=ALU.mult), [("ztmp", q2), bzk], [("zt", ti)])
                                for j in range(4):
                                    silus[j]()
                                    to_tokmajor(xc[:, j, :], [("xc", j)], x_tok[:, :, j * 128:(j + 1) * 128], ["x_tok"])
                                silus[4]()
                                to_tokmajor(BTg, ["BTg"], Btok[:, :, :], ["Btok"])
                                silus[5]()
                                T.barrier()

                            with contextlib.ExitStack() as sd:
                                def sbd(name, shape, dt=F32):
                                    return sb(name, shape, dt, sd)
                                xdt = sbd("xdt", [128, 3, 512], BF16)
                                T1 = sbd("T1", [128, 2, 512])
                                cbT = sbd("cbT", [128, 2, 128], BF16)
                                decT = sbd("decT", [128, 2, 4, 128], BF16)
                                MT = sbd("MT", [128, 2, 4, 128], BF16)
                                ygs = sbd("ygs", [128, nt, 512], BF16)
                                ssg = sbd("ssg", [128, nt])
                                junk = sbd("junk", [128, 512], BF16)
                                xdd = sbd("xdd", [128, 2, 512], BF16)
                                yn = sbd("yn", [128, 2, 512], BF16)
                                hout = sbd("hout", [128, 4, 128])
                                snorm_g = sbd("snorm_g", [128, 512])
                                T.dma("sp", snorm_g[:], ssd_norm[g * 512:(g + 1) * 512].partition_broadcast(128),
                                      writes=["snorm_g"], group=("sng", 0))
                                NH0 = 6
                                if has_s:
                                    CTm = sbd("CTm", [128, NSEQ, 128], BF16)
                                    h0 = sbd("h0", [128, NH0, 4, 128])
                                    h0T = sbd("h0T", [128, 2, 512], BF16)
                                    Btm = sbd("Btm", [128, 2, 128], BF16)
                                    dtAx = sbd("dtAx", [128, 512])
                                    dcol = sbd("dcol", [128, 4, NSEQ])
                                hc = [0]
                                bys = {}

                                def h3(ap):
                                    return ap.rearrange("p (h q) -> p h q", q=HP)

                                def ph0(ti, t):
                                    kind = 1 if t == STILE else 0
                                    cols = slice(ti * 128, (ti + 1) * 128)
                                    s3, s2 = ti % 3, ti % 2
                                    bcb, bcbk = nbf()
                                    mm_group(bcb[:, 0:128], [(BTg[:, cols], CTg[:, cols])], ["BTg", "CTg"], bcbk)
                                    T.op("act", lambda: A_.copy(out=cbT[:, s2, :], in_=bcb[:, 0:128]),
                                         [bcbk], [("cbT", s2)])
                                    xt3 = h3(x_tok[:, ti, :])
                                    T.op("pool", lambda: P_.tensor_tensor(
                                        out=h3(xdt[:, s3, :]), in0=xt3, in1=bc_mid(dt_[:, ti, hs0:hs0 + 8], HP),
                                        op=ALU.mult), ["x_tok", "dt_"], [("xdt", s3)])

                                def ph1(ti, t):
                                    kind = 1 if t == STILE else 0
                                    s3, s2 = ti % 3, ti % 2
                                    by, byk = nbf(hold=True)
                                    bys[ti] = (by, byk)
                                    brs = []
                                    for half in range(2):
                                        h4 = 4 * half
                                        br, brk = nbf(hold=True)
                                        brs.append((br, brk))
                                        rd = ["ones_b", "ident_b", "negm", "hi", "lo", "tri_b"]
                                        T.op("pe", lambda br=br: nc.tensor.matmul(
                                            br[:], lhsT=ident_b[:], rhs=negm[:, kind].rearrange("p a b -> p (a b)"),
                                            start=True, stop=False), rd, [brk], inc=False)
                                        for hh in range(4):
                                            hg = hs0 + h4 + hh
                                            T.op("pe", lambda br=br, hh=hh, hg=hg: nc.tensor.matmul(
                                                br[:, hh * 128:(hh + 1) * 128],
                                                lhsT=hi[:, ti, hg:hg + 1].to_broadcast([128, 128]),
                                                rhs=tri_b[:, kind, :], start=False, stop=False), rd, [brk], inc=False)
                                            T.op("pe", lambda br=br, hh=hh, hg=hg: nc.tensor.matmul(
                                                br[:, hh * 128:(hh + 1) * 128],
                                                lhsT=lo[:, ti, hg:hg + 1].to_broadcast([128, 128]),
                                                rhs=tri_b[:, kind, :], start=False, stop=(hh == 3)), rd, [brk],
                                                inc=(hh == 3))
                                    for half in range(2):
                                        h4 = 4 * half
                                        br, brk = brs[half]
                                        for hh in range(4):
                                            hg = hs0 + h4 + hh
                                            T.op("act", lambda br=br, hh=hh, hg=hg, half=half: A_.activation(
                                                out=decT[:, half, hh, :], in_=br[:, hh * 128:(hh + 1) * 128], func=AF.Exp,
                                                bias=nacs[:, ti, hg:hg + 1], scale=1.0), [brk, "nacs"], [("decT", half)])
                                        release(brk)
                                        T.op("dve", lambda half=half: V.tensor_tensor(
                                            out=MT[:, half], in0=decT[:, half], in1=bc_out(cbT[:, s2, :], 4), op=ALU.mult),
                                            [("decT", half), ("cbT", s2)], [("MT", half)])
                                    for half in range(2):
                                        h4 = 4 * half
                                        for hh in range(4):
                                            hl = h4 + hh
                                            T.op("pe", lambda hh=hh, hl=hl, half=half: nc.tensor.matmul(
                                                by[:, hl * 64:(hl + 1) * 64], lhsT=MT[:, half, hh, :],
                                                rhs=xdt[:, s3, hl * 64:(hl + 1) * 64], start=True, stop=False),
                                                [("MT", half), ("xdt", s3)], [byk], inc=False)
                                            T.op("pe", lambda hl=hl: nc.tensor.matmul(
                                                by[:, hl * 64:(hl + 1) * 64], lhsT=Ddiag[:, hs0 + hl, :],
                                                rhs=x_tok[:, ti, hl * 64:(hl + 1) * 64], start=False, stop=True),
                                                ["Ddiag", "x_tok"], [byk], inc=True)

                                def ph2(ti, t):
                                    kind = 1 if t == STILE else 0
                                    cols = slice(ti * 128, (ti + 1) * 128)
                                    s3, s2 = ti % 3, ti % 2
                                    kxdt = ("xdt", s3)
                                    kT1 = ("T1", s2)
                                    kxdd = ("xdd", s2)
                                    T.op("pool", lambda: P_.tensor_tensor(
                                        out=h3(xdd[:, s2, :]), in0=h3(xdt[:, s3, :]),
                                        in1=bc_mid(dend[:, ti, hs0:hs0 + 8], HP), op=ALU.mult),
                                        [kxdt, "dend"], [kxdd])
                                    if kind == 0:
                                        ba, bak = nbf()
                                        mm_group(ba[:], [(CTg[:, cols], hTb[:, g, :])], ["CTg", ("hTb", g)], bak)
                                    else:
                                        ba, bak = nbf(hold=True)

                                        def ld(s):
                                            sl = s % NH0
                                            T.dma("sp", h0[:, sl].rearrange("p j n -> p (j n)"),
                                                  st_ssm[s, g * 512:(g + 1) * 512, :].rearrange("(p j) n -> p (j n)", j=4),
                                                  writes=[("h0", sl)], group=("h0", sl))

                                        def sA(s):
                                            sl = s % NH0
                                            q2 = s % 2
                                            bt, btk = nbf()
                                            tr_group([(bt[:, j * 128:(j + 1) * 128], h0[:, sl, j, :]) for j in range(4)],
                                                     ident_f[:], [("h0", sl), "ident_f"], btk)
                                            T.op("act", lambda bt=bt, q2=q2: A_.copy(
                                                out=h0T[:, q2, :].rearrange("n (p j) -> n j p", j=4),
                                                in_=bt[:].rearrange("n (j p) -> n j p", j=4)), [btk], [("h0T", q2)])
                                            T.op("act", lambda s=s, q2=q2: A_.activation(
                                                out=Btm[:, q2, :], in_=Btok[:, ti, :], func=AF.Copy,
                                                scale=rowmask[:, s:s + 1]), ["Btok", "rowmask"], [("Btm", q2)])

                                        def sB(s):
                                            sl = s % NH0
                                            q2 = s % 2
                                            kh = ("h0", sl)
                                            T.op("pe", lambda s=s, q2=q2: nc.tensor.matmul(
                                                ba[:], lhsT=CTm[:, s, :], rhs=h0T[:, q2, :], start=(s == 0),
                                                stop=(s == NSEQ - 1)), ["CTm", ("h0T", q2)], [bak])
                                            bs_, bsk = nbf()
                                            for j in range(4):
                                                mm_group(bs_[:, j * 128:(j + 1) * 128],
                                                         [(xdd[:, s2, j:512:4], Btm[:, q2, :])],
                                                         [kxdd, ("Btm", q2)], bsk)
                                            T.op("dve", lambda s=s, sl=sl, bs_=bs_: V.scalar_tensor_tensor(
                                                out=h0[:, sl].rearrange("p j n -> p (j n)"),
                                                in0=h0[:, sl].rearrange("p j n -> p (j n)"), scalar=dcol[:, 0, s:s + 1],
                                                in1=bs_[:], op0=ALU.mult, op1=ALU.add), [kh, "dcol", bsk], [kh])
                                            T.dma("sp", o_ssm_s[s, g * 512:(g + 1) * 512, :].rearrange(
                                                "(p j) n -> p (j n)", j=4), h0[:, sl].rearrange("p j n -> p (j n)"),
                                                reads=[kh], group=("h0o", sl))

                                        for s in range(4):
                                            ld(s)
                                        sA(0)
                                        for s in range(NSEQ):
                                            if s + 4 < NSEQ:
                                                ld(s + 4)
                                            if s + 1 < NSEQ:
                                                sA(s + 1)
                                            sB(s)
                                    T.op("dve", lambda: V.tensor_tensor(
                                        out=h3(T1[:, s2, :]), in0=h3(ba[:]),
                                        in1=bc_mid(eacs[:, ti, hs0:hs0 + 8], HP), op=ALU.mult),
                                        [bak, "eacs"], [kT1])
                                    if kind == 1:
                                        release(bak)
                                    by, byk = bys.pop(ti)
                                    T.op("dve", lambda: V.tensor_tensor(out=T1[:, s2, :], in0=by[:], in1=T1[:, s2, :],
                                                                        op=ALU.add), [byk, kT1], [kT1])
                                    release(byk)
                                    T.op("dve", lambda: V.tensor_tensor(out=T1[:, s2, :], in0=T1[:, s2, :],
                                                                        in1=zt_all[:, ti, :], op=ALU.mult),
                                         [kT1, ("zt", ti)], [kT1])
                                    T.op("act", lambda: A_.activation(out=junk[:], in_=T1[:, s2, :], func=AF.Square,
                                                                      accum_out=ssg[:, ti:ti + 1]),
                                         [kT1], ["junk", "ssg"])
                                    T.op("pool", lambda: P_.tensor_copy(out=ygs[:, ti, :], in_=T1[:, s2, :]),
                                         [kT1], ["ygs"])
                                    if kind == 0:
                                        bs_, bsk = nbf()
                                        mm_group(bs_[:], [(Btok[:, ti, :], xdd[:, s2, :])], ["Btok", kxdd], bsk)
                                        hv = h3(hT[:, g, :])
                                        T.op("dve", lambda: V.tensor_tensor(
                                            out=hv, in0=hv, in1=bc_mid(decb[:, ti, hs0:hs0 + 8], HP), op=ALU.mult),
                                            [("hT", g), "decb"], [("hT", g)])
                                        T.op("dve", lambda: V.tensor_tensor(
                                            out=hT[:, g, :], in0=hT[:, g, :], in1=bs_[:], op=ALU.add),
                                            [("hT", g), bsk], [("hT", g)])
                                        T.op("pool", lambda: P_.tensor_copy(out=hTb[:, g, :], in_=hT[:, g, :]),
                                             [("hT", g)], [("hTb", g)])
                                        if t == NPT - 1:
                                            bt, btk = nbf()
                                            tr_group([(bt[:, j * 128:(j + 1) * 128], hT[:, g, j * 128:(j + 1) * 128])
                                                      for j in range(4)], ident_f[:], [("hT", g), "ident_f"], btk)
                                            T.op("act", lambda: A_.copy(
                                                out=hout[:], in_=bt[:].rearrange("p (j n) -> p j n", n=128)),
                                                [btk], ["hout"])
                                            T.dma("sp", o_ssm_p[g * 512:(g + 1) * 512, :].rearrange(
                                                "(j p) n -> p j n", p=128), hout[:], reads=["hout"], group=("hout", 0))

                                if has_s:
                                    def sample_setup():
                                        ti = nt - 1
                                        cols = slice(ti * 128, (ti + 1) * 128)
                                        T.op("pool", lambda: P_.tensor_copy(
                                                out=h3(dtAx[:]), in_=bc_mid(dtA[:, ti, hs0:hs0 + 8], HP)), ["dtA"], ["dtAx"])
                                        bd, bdk = nbf()
                                        mm_group(bd[:, 0:NSEQ], [(dtAx[:, 0:512:4], rowmask[:, :])], ["dtAx", "rowmask"], bdk)
                                        T.op("act", lambda: A_.activation(out=dcol[:, 0, :], in_=bd[:, 0:NSEQ], func=AF.Exp),
                                                 [bdk], ["dcol"])
                                        T.op("pool", lambda: P_.tensor_tensor(
                                                out=CTm[:], in0=bc_out(CTg[:, cols], NSEQ), in1=smask[:], op=ALU.mult),
                                                ["CTg", "smask"], ["CTm"])
                                    sample_setup()
                                for step in range(nt + 2):
                                    if step < nt:
                                        ph0(step, tiles[step])
                                    if 0 <= step - 2 < nt:
                                        ph2(step - 2, tiles[step - 2])
                                    if 0 <= step - 1 < nt:
                                        ph1(step - 1, tiles[step - 1])
                                T.op("act", lambda: A_.activation(out=ssg[:], in_=ssg[:], func=AF.Sqrt,
                                                                  scale=1.0 / 512, bias=eps_col[:, 1:2]),
                                     ["ssg", "eps_col"], ["ssg"])
                                T.op("dve", lambda: V.reciprocal(out=ssg[:], in_=ssg[:]), ["ssg"], ["ssg"])
                                for ti in range(nt):
                                    s2 = ti % 2
                                    T.op("dve", lambda ti=ti, s2=s2: V.scalar_tensor_tensor(
                                        out=yn[:, s2, :], in0=ygs[:, ti, :], scalar=ssg[:, ti:ti + 1], in1=snorm_g[:],
                                        op0=ALU.mult, op1=ALU.mult), ["ygs", "ssg", "snorm_g"], [("yn", s2)])
                                    bb, bbk = nbb()
                                    tr_group([(bb[:, j * 128:(j + 1) * 128], yn[:, s2, j * 128:(j + 1) * 128])
                                              for j in range(4)], ident_b[:], [("yn", s2), "ident_b"], bbk)
                                    T.op("act", lambda bb=bb, ti=ti: A_.copy(
                                        out=ynT[:, 4 * g:4 * g + 4, ti * 128:(ti + 1) * 128],
                                        in_=bb[:, 0:512].rearrange("p (j t) -> p j t", t=128)), [bbk], [("ynT", ti)])
                                T.barrier()
                        T.barrier()

                    with contextlib.ExitStack() as sbs:
                        yp = sb("yp", [128, 8, NTMAX], BF16, sbs)
                        mrg = sb("mrg", [128, 8, NTMAX], BF16, sbs)
                        with contextlib.ExitStack() as sp1:
                            uext = sb("uext", [128, 2, 15 + NTMAX], F32, sp1)
                            swa = sb("swa", [128, 15 + NTMAX], F32, sp1)
                            swb = sb("swb", [128, 15 + NTMAX], F32, sp1)
                            dg = sb("dg", [128, 2, NTMAX], BF16, sp1)
                            fixb = sb("fixb", [128, 16], F32, sp1)
                            wgrp_t = sb("wgrp_t", [128, 8, 256], BF16, sp1)
                            T.dma("pool", wgrp_t[:], pool_w_group.rearrange("g (kc p) n -> p (g kc) n", p=128),
                                  writes=["wgrp_t"], group=("wgrp", 0))
                            if has_s:
                                sphist = sb("sphist", [128, 8, NSEQ, 15], F32, sp1)
                                stp = sb("stp", [128, 2, 512], F32, sp1)
                                uexs = sb("uexs", [128, 2, NSEQ, 15 + TS], F32, sp1)
                                swsa = sb("swsa", [128, NSEQ, 15 + TS], F32, sp1)
                                swsb = sb("swsb", [128, NSEQ, 15 + TS], F32, sp1)
                                cnt2 = 0
                                for hh in range(2):
                                    for cb in range(2):
                                        s2 = cnt2 % 2
                                        cnt2 += 1
                                        T.dma("sp", stp[0:120, s2, :],
                                              st_pool[hh * 8:(hh + 1) * 8, :, cb * 512:(cb + 1) * 512].rearrange(
                                                  "s r c -> (s r) c"), writes=[("stp", s2)], group=("stp", s2))
                                        bank, bkey = nbf()
                                        tr_group([(bank[:, j * 120:(j + 1) * 120], stp[0:120, s2, j * 128:(j + 1) * 128])
                                                  for j in range(4)], ident_f[0:120, 0:120], [("stp", s2), "ident_f"],
                                                 bkey)
                                        T.op("act", lambda bank=bank, hh=hh, cb=cb: A_.copy(
                                            out=sphist[:, cb * 4:(cb + 1) * 4, hh * 8:(hh + 1) * 8, :].rearrange(
                                                "p j s r -> p j (s r)"),
                                            in_=bank[:, 0:480].rearrange("p (j q) -> p j q", j=4)), [bkey], ["sphist"])
                                T.dma("sp", o_pool_s[:, 0:7, :], st_pool[:, 8:15, :], group=("pcp", 0))
                            wgrp = wgrp_t
                            wkg = ["wgrp_t"]
                            ucnt = 0
                            for ub in range(4):
                                win = 2 << ub
                                view, wk = w_next("u")
                                wu_ = view(0, 8, 256)
                                for sub in range(2):
                                    c = ub * 2 + sub
                                    s2 = ucnt % 2
                                    ucnt += 1
                                    ku = ("uext", s2)
                                    for (p0, pn) in pieces:
                                        bank, bkey = nbf()
                                        mm_group(bank[:, 0:pn], [(wu_[:, kc, sub * 128:(sub + 1) * 128],
                                                                  xn[:, kc, p0:p0 + pn]) for kc in range(KC)],
                                                 wk + tkeys("xn", p0, pn), bkey)
                                        if p0 < NTp:
                                            T.op("act", lambda bank=bank, p0=p0, pn=pn, s2=s2: A_.copy(
                                                out=uext[:, s2, 15 + p0:15 + p0 + pn], in_=bank[:, 0:pn]), [bkey], [ku])
                                        else:
                                            T.op("act", lambda bank=bank, s2=s2: A_.copy(
                                                out=uexs[:, s2, :, 15:15 + TS],
                                                in_=bank[:, 0:128].rearrange("p (s t) -> p s t", t=TS)), [bkey], [ku])
                                    if npt:
                                        T.op("pool", lambda s2=s2, c=c: P_.tensor_copy(out=uext[:, s2, 0:15],
                                                                                      in_=poolhist[:, c, :]),
                                             [("poolhist", c)], [ku])
                                        W_ = 15 + NTp
                                        src = uext[:, s2, 0:W_]
                                        bufs = [swa, swb]
                                        sh = 1
                                        bi = 0
                                        srck = ku
                                        while sh < win:
                                            dst = bufs[bi][:, 0:W_]
                                            T.op("dve", lambda dst=dst, src=src, sh=sh: V.tensor_tensor(
                                                out=dst[:, sh:W_], in0=src[:, sh:W_], in1=src[:, 0:W_ - sh], op=ALU.add),
                                                [srck], [("sw", bi)])
                                            src = dst
                                            srck = ("sw", bi)
                                            bi ^= 1
                                            sh *= 2
                                        T.op("dve", lambda src=src, s2=s2, sub=sub: V.scalar_tensor_tensor(
                                            out=dg[:, sub, 0:NTp], in0=src[:, 15:15 + NTp], scalar=1.0 / win,
                                            in1=uext[:, s2, 15:15 + NTp], op0=ALU.mult, op1=ALU.subtract),
                                            [srck, ku], [("dg", sub)])
                                        if first_pass:
                                            nfix = win - 1
                                            T.op("dve", lambda src=src, nfix=nfix: V.tensor_tensor(
                                                out=fixb[:, 0:nfix], in0=src[:, 15:15 + nfix], in1=invc[:, 0:nfix],
                                                op=ALU.mult), [srck, "invc"], ["fixb"])
                                            T.op("dve", lambda s2=s2, sub=sub, nfix=nfix: V.tensor_tensor(
                                                out=dg[:, sub, 0:nfix], in0=fixb[:, 0:nfix],
                                                in1=uext[:, s2, 15:15 + nfix], op=ALU.subtract),
                                                ["fixb", ku], [("dg", sub)])
                                        if more_prompt:
                                            T.op("pool", lambda s2=s2, c=c: P_.tensor_copy(
                                                out=poolhist[:, c, :], in_=uext[:, s2, NTp:NTp + 15]),
                                                [ku], [("poolhist", c)])
                                    if has_s:
                                        T.op("pool", lambda s2=s2, c=c: P_.tensor_copy(
                                            out=uexs[:, s2, :, 0:15], in_=sphist[:, c, :, :]), ["sphist"], [ku])
                                        W_ = 15 + TS
                                        src = uexs[:, s2]
                                        bufs = [swsa, swsb]
                                        sh = 1
                                        bi = 0
                                        srck = ku
                                        while sh < win:
                                            dst = bufs[bi]
                                            T.op("pool", lambda dst=dst, src=src, sh=sh: P_.tensor_tensor(
                                                out=dst[:, :, sh:W_], in0=src[:, :, sh:W_], in1=src[:, :, 0:W_ - sh],
                                                op=ALU.add), [srck], [("sws", bi)])
                                            src = dst[:]
                                            srck = ("sws", bi)
                                            bi ^= 1
                                            sh *= 2
                                        T.op("dve", lambda src=src, s2=s2, sub=sub: V.scalar_tensor_tensor(
                                            out=dg[:, sub, NTp:NT].rearrange("p (s t) -> p s t", t=TS),
                                            in0=src[:, :, 15:15 + TS], scalar=1.0 / win,
                                            in1=uexs[:, s2, :, 15:15 + TS], op0=ALU.mult, op1=ALU.subtract),
                                            [srck, ku], [("dg", sub)])
                                tokmajor_out(wu_, wk, 256, "pool", ub * 256)
                                for m2 in range(2):
                                    m = ub * 2 + m2
                                    for (p0, pn) in pieces:
                                        bank, bkey = nbf()
                                        mm_group(bank[:, 0:pn], [(wgrp[:, ub * 2 + k2, m2 * 128:(m2 + 1) * 128],
                                                                  dg[:, k2, p0:p0 + pn]) for k2 in range(2)],
                                                 wkg + [("dg", 0), ("dg", 1)], bkey)
                                        T.op("act", lambda bank=bank, m=m, p0=p0, pn=pn: A_.activation(
                                            out=yp[:, m, p0:p0 + pn], in_=bank[:, 0:pn], func=AF.Copy,
                                            scale=pscol[:, m:m + 1]), [bkey, "pscol"], [("yp", m)])
                            T.barrier()
                        with contextlib.ExitStack() as sp2:
                            gs = sb("gs", [128, 2, NTMAX], F32, sp2)
                            m1 = sb("m1", [128, 2, NTMAX], F32, sp2)

                            def gate(tag):
                                view, wk = w_next(tag)
                                wgt = view(0, 8, 256)
                                for sub in range(2):
                                    for (p0, pn) in pieces:
                                        bank, bkey = nbf()
                                        mm_group(bank[:, 0:pn], [(wgt[:, kc, sub * 128:(sub + 1) * 128],
                                                                  xn[:, kc, p0:p0 + pn]) for kc in range(KC)],
                                                 wk + tkeys("xn", p0, pn), bkey)
                                        T.op("act", lambda bank=bank, sub=sub, p0=p0, pn=pn: A_.activation(
                                            out=gs[:, sub, p0:p0 + pn], in_=bank[:, 0:pn], func=AF.Sigmoid),
                                            [bkey], [("gs", sub)])
                            for mb in range(4):
                                gate("g1")
                                view, wk = w_next("sso")
                                wso = view(0, 16, 256)
                                for sub in range(2):
                                    for (p0, pn) in pieces:
                                        bank, bkey = nbf()
                                        mm_group(bank[:, 0:pn], [(wso[:, cc, sub * 128:(sub + 1) * 128],
                                                                  ynT[:, cc, p0:p0 + pn]) for cc in range(16)],
                                                 wk + tkeys("ynT", p0, pn), bkey)
                                        T.op("dve", lambda bank=bank, sub=sub, p0=p0, pn=pn: V.tensor_tensor(
                                            out=m1[:, sub, p0:p0 + pn], in0=bank[:, 0:pn], in1=gs[:, sub, p0:p0 + pn],
                                            op=ALU.mult), [bkey, ("gs", sub)], [("m1", sub)])
                                gate("g0")
                                view, wk = w_next("pwo")
                                wpo = view(0, 8, 256)
                                for sub in range(2):
                                    m = mb * 2 + sub
                                    for (p0, pn) in pieces:
                                        bank, bkey = nbf()
                                        mm_group(bank[:, 0:pn], [(wpo[:, kc, sub * 128:(sub + 1) * 128],
                                                                  yp[:, kc, p0:p0 + pn]) for kc in range(KC)],
                                                 wk + [("yp", kc) for kc in range(KC)], bkey)
                                        T.op("dve", lambda bank=bank, sub=sub, p0=p0, pn=pn: V.tensor_tensor(
                                            out=gs[:, sub, p0:p0 + pn], in0=bank[:, 0:pn], in1=gs[:, sub, p0:p0 + pn],
                                            op=ALU.mult), [bkey, ("gs", sub)], [("gs", sub)])
                                        T.op("pool", lambda sub=sub, m=m, p0=p0, pn=pn: P_.tensor_tensor(
                                            out=mrg[:, m, p0:p0 + pn], in0=gs[:, sub, p0:p0 + pn],
                                            in1=m1[:, sub, p0:p0 + pn], op=ALU.add),
                                            [("gs", sub), ("m1", sub)], [("mrg", m)])
                            for mb in range(4):
                                view, wk = w_next("wo")
                                wo_ = view(0, 8, 256)
                                for sub in range(2):
                                    m = mb * 2 + sub
                                    for (p0, pn) in pieces:
                                        bank, bkey = nbf()
                                        mm_group(bank[:, 0:pn], [(wo_[:, kc, sub * 128:(sub + 1) * 128],
                                                                  mrg[:, kc, p0:p0 + pn]) for kc in range(KC)],
                                                 wk + [("mrg", kc) for kc in range(KC)], bkey)
                                        xk = tkeys("xT", p0, pn)
                                        T.op("dve", lambda bank=bank, m=m, p0=p0, pn=pn: V.tensor_tensor(
                                            out=xT[:, m, p0:p0 + pn], in0=bank[:, 0:pn], in1=xT[:, m, p0:p0 + pn],
                                            op=ALU.add), [bkey] + xk, xk)
                            T.barrier()
                        T.barrier()
                    T.barrier()

            if cfg["ffn1"]:
                with contextlib.ExitStack() as scope:
                    ffn(0, scope)
                    T.barrier()

            if cfg["mix"]:
                mix()

            if cfg["ffn2"]:
                with contextlib.ExitStack() as scope:
                    ffn(2, scope)
                    T.barrier()

            with contextlib.ExitStack() as scope:
                yo = sb("yo", [128, 2, D], F32, scope)
                junk2 = sb("junk2", [128, 512], F32, scope)
                ssq = sb("ssq", [128, 2, 4], F32, scope)
                gfin_b = sb("gfin_b", [128, D], F32, scope)
                T.dma("sp", gfin_b[:], norm_final.partition_broadcast(128), writes=["gfin_b"], group=("gfin", 0))
                fb = {}

                def fA(ti):
                    s2 = ti % 2
                    banks = []
                    for half in range(2):
                        bank, bkey = nbf(hold=True)
                        tr_group([(bank[:, c4 * 128:(c4 + 1) * 128], xT[:, half * 4 + c4, ti * 128:(ti + 1) * 128])
                                  for c4 in range(4)], ident_f[:], [("xT", ti), "ident_f"], bkey)
                        T.op("act", lambda bank=bank, half=half, s2=s2: A_.activation(
                            out=junk2[:], in_=bank[:], func=AF.Square, accum_out=ssq[:, s2, half:half + 1]),
                            [bkey], ["junk2", ("ssq", s2)])
                        banks.append((bank, bkey))
                    fb[ti] = banks
                    T.op("dve", lambda s2=s2: V.tensor_tensor(out=ssq[:, s2, 2:3], in0=ssq[:, s2, 0:1],
                                                              in1=ssq[:, s2, 1:2], op=ALU.add),
                         [("ssq", s2)], [("ssq", s2)])

                def fB(ti):
                    tile = tiles[ti]
                    s2 = ti % 2
                    banks = fb.pop(ti)
                    T.op("act", lambda s2=s2: A_.activation(out=ssq[:, s2, 3:4], in_=ssq[:, s2, 2:3], func=AF.Sqrt,
                                                            scale=1.0 / D, bias=eps_col[:, 0:1]),
                         [("ssq", s2), "eps_col"], [("ssq", s2)])
                    T.op("dve", lambda s2=s2: V.reciprocal(out=ssq[:, s2, 3:4], in_=ssq[:, s2, 3:4]),
                         [("ssq", s2)], [("ssq", s2)])
                    for half in range(2):
                        bank, bkey = banks[half]
                        T.op("dve", lambda bank=bank, half=half, s2=s2: V.scalar_tensor_tensor(
                            out=yo[:, s2, half * 512:(half + 1) * 512], in0=bank[:], scalar=ssq[:, s2, 3:4],
                            in1=gfin_b[:, half * 512:(half + 1) * 512], op0=ALU.mult, op1=ALU.mult),
                            [bkey, ("ssq", s2), "gfin_b"], [("yo", s2)])
                        release(bkey)
                    dst = y_p[tile * 128:(tile + 1) * 128, :] if tile < NPT else y_s[:, :]
                    T.dma("sp", dst, yo[:, s2, :], reads=[("yo", s2)], group=("yo", s2))

                fA(0)
                for ti in range(nt):
                    if ti + 1 < nt:
                        fA(ti + 1)
                    fB(ti)
                T.barrier()

        for pi, tiles in enumerate(PASSES):
            emit_pass(pi, tiles)
        T.final_wait()
        build_nc.stats = (T.n_ops, T.n_wait, T.nsem)
    return nc


_IN_NAMES = ["norm_ffn1", "ffn1_w_in", "ffn1_w_out", "norm_mix", "w_in", "pool_w_group", "pool_scale",
             "pool_w_out", "conv_w", "conv_b", "dt_bias", "a_log", "d_skip", "ssd_norm", "ssd_w_out", "w_o",
             "norm_ffn2", "ffn2_w_in", "ffn2_w_out"]


def kernel(**inputs):
    n = 8
    f = lambda a: np.ascontiguousarray(np.asarray(a, dtype=np.float32))
    shared = {k: f(inputs[k])[0] for k in _IN_NAMES}
    shared["norm_final"] = f(inputs["norm_final"])
    x_prompt = f(inputs["x_prompt"])
    x_sample = f(inputs["x_sample"])
    state_pool = f(inputs["state_pool"])[0]
    state_conv = f(inputs["state_conv"])[0]
    state_ssm = f(inputs["state_ssm"])[0]
    in_maps = []
    for c in range(n):
        m = dict(shared)
        m["xp"] = x_prompt[c]
        m["xs"] = np.ascontiguousarray(x_sample[c * NSEQ:(c + 1) * NSEQ].reshape(NSEQ * TS, D))
        m["st_pool"] = np.ascontiguousarray(state_pool[c * NSEQ:(c + 1) * NSEQ])
        m["st_conv"] = np.ascontiguousarray(state_conv[c * NSEQ:(c + 1) * NSEQ])
        m["st_ssm"] = np.ascontiguousarray(state_ssm[c * NSEQ:(c + 1) * NSEQ].reshape(NSEQ, DIN, NST))
        in_maps.append(m)
    nc = build_nc()
    res = run_bass_kernel_spmd(nc, in_maps, core_ids=list(range(n)))
    R = res.results
    y_prompt = np.stack([R[c]["y_p"] for c in range(n)], 0)
    y_sample = np.concatenate([R[c]["y_s"].reshape(NSEQ, TS, D) for c in range(n)], 0)
    pool_p = np.stack([R[c]["o_pool_p"] for c in range(n)], 0)[None]
    conv_p = np.stack([R[c]["o_conv_p"] for c in range(n)], 0)[None]
    ssm_p = np.stack([R[c]["o_ssm_p"].reshape(NH, HP, NST) for c in range(n)], 0)[None]
    pool_s = np.concatenate([R[c]["o_pool_s"] for c in range(n)], 0)[None]
    conv_s = np.concatenate([R[c]["o_conv_s"] for c in range(n)], 0)[None]
    ssm_s = np.concatenate([R[c]["o_ssm_s"].reshape(NSEQ, NH, HP, NST) for c in range(n)], 0)[None]
    return (y_prompt, y_sample, pool_p, conv_p, ssm_p, pool_s, conv_s, ssm_s)
```
